# Optimizing a Trainium2 kernel written in Bass

```python
import jax, jax.numpy as jnp
from jax import lax
import numpy as np

D_MODEL = 2048
BATCH = 4
SEQ = 4096
DEPTH = 2

N_MIXERS = 2
HGRN_HEADS = 16
HGRN_DK = D_MODEL // HGRN_HEADS
HGRN_DV = D_MODEL // HGRN_HEADS
HGRN_CHUNK = 64
GMLP_CHUNK = 128
GMLP_GROUPS = 16
GMLP_GDIM = D_MODEL // GMLP_GROUPS
D_FF = 4 * D_MODEL
CONV_W = 3
PLE_DIM = 256
EPS = 1e-6
N_HGRN = (DEPTH + 1) // 2
N_GMLP = DEPTH // 2

kernel_name = 'hybrid_hgrn2_gmlp_convffn_trunk'


def rms_norm(x, g):
    xf = x.astype(jnp.float32)
    y = xf * lax.rsqrt(jnp.mean(xf * xf, axis=-1, keepdims=True) + EPS)
    return (y * g.astype(jnp.float32)).astype(x.dtype)


def layer_norm(x, g, b):
    xf = x.astype(jnp.float32)
    mu = jnp.mean(xf, axis=-1, keepdims=True)
    var = jnp.mean(jnp.square(xf - mu), axis=-1, keepdims=True)
    y = (xf - mu) * lax.rsqrt(var + EPS)
    return (y * g.astype(jnp.float32) + b.astype(jnp.float32)).astype(x.dtype)


def hgrn2_chunk_scan(q, k, v, logf):
    B, S, H, DK = q.shape
    DV = v.shape[-1]
    C = HGRN_CHUNK
    N = S // C

    def to_chunks(t):
        return t.astype(jnp.float32).reshape(B, N, C, H, t.shape[-1]).transpose(1, 0, 3, 2, 4)

    qc, kc, vc, gc = to_chunks(q), to_chunks(k), to_chunks(v), to_chunks(logf)
    causal = jnp.tril(jnp.ones((C, C), dtype=bool))[:, :, None]

    def step(state, inp):
        qi, ki, vi, gi = inp
        b = jnp.cumsum(gi, axis=2)
        diff = b[:, :, :, None, :] - b[:, :, None, :, :]
        decay = jnp.exp(jnp.where(causal, diff, -jnp.inf))
        scores = jnp.einsum('bhtsk,bhsk->bhts', qi[:, :, :, None, :] * decay, ki)
        o = (jnp.einsum('bhts,bhsv->bhtv', scores, vi)
             + jnp.einsum('bhtk,bhkv->bhtv', qi * jnp.exp(b), state))
        b_last = b[:, :, -1:, :]
        k_dec = ki * jnp.exp(b_last - b)
        new_state = (state * jnp.exp(b_last[:, :, 0, :])[..., None]
                     + jnp.einsum('bhsk,bhsv->bhkv', k_dec, vi))
        return new_state, o

    init = jnp.zeros((B, H, DK, DV), jnp.float32)
    _, o = lax.scan(step, init, (qc, kc, vc, gc))
    return o.transpose(1, 0, 3, 2, 4).reshape(B, S, H, DV)


def hgrn2_mixer(h, w_in, lb, norm_g, w_out):
    B, S, _ = h.shape
    proj = h @ w_in
    q, fz, i, g = jnp.split(proj, 4, axis=-1)
    heads = lambda t: t.reshape(B, S, HGRN_HEADS, -1)
    lbh = lb.astype(jnp.float32).reshape(HGRN_HEADS, HGRN_DK)
    f = lbh + (1.0 - lbh) * jax.nn.sigmoid(heads(fz).astype(jnp.float32))
    o = hgrn2_chunk_scan(heads(q), 1.0 - f, heads(i), jnp.log(f)).astype(h.dtype)
    o = rms_norm(o, norm_g) * jax.nn.silu(heads(g))
    return o.reshape(B, S, D_MODEL) @ w_out


def gmlp_mixer(h, w_in, ln_g, ln_b, w_s, b_s, w_out):
    B, S, _ = h.shape
    N = S // GMLP_CHUNK
    z = jax.nn.gelu(h @ w_in, approximate=False)
    u, v = jnp.split(z, 2, axis=-1)
    v = layer_norm(v, ln_g, ln_b)
    vc = v.reshape(B, N, GMLP_CHUNK, GMLP_GROUPS, GMLP_GDIM)
    ws = w_s * jnp.tril(jnp.ones((GMLP_CHUNK, GMLP_CHUNK), w_s.dtype))
    mixed = jnp.einsum('gts,bnsgc->bntgc', ws, vc) + b_s.T[:, :, None]
    return (u * mixed.reshape(B, S, D_MODEL)) @ w_out


def conv_ffn(h, w_up, conv_w, conv_b, w_down):
    S = h.shape[1]
    hid = h @ w_up
    padded = jnp.pad(hid, ((0, 0), (CONV_W - 1, 0), (0, 0)))
    acc = conv_b
    for j in range(CONV_W):
        acc = acc + padded[:, j:j + S] * conv_w[j]
    gate, val = jnp.split(acc, 2, axis=-1)
    return (jax.nn.gelu(gate, approximate=True) * val) @ w_down


def setup_inputs(seed: int = 0) -> dict:
    key = jax.random.key(seed)
    ks = jax.random.split(key, 20)
    D = D_MODEL
    nrm = lambda k, shape, scale: jax.random.normal(k, shape, jnp.float32) * scale
    return {
        'x': nrm(ks[0], (BATCH, SEQ, D), 1.0),
        'p': nrm(ks[1], (DEPTH, BATCH, SEQ, PLE_DIM), 1.0),
        'norm_g': 1.0 + nrm(ks[2], (DEPTH, 4, D), 0.02),
        'hgrn_w_in': nrm(ks[3], (N_HGRN, D, 4 * D), D ** -0.5),
        'hgrn_lb_logits': 1.0 + nrm(ks[4], (DEPTH + 1, D), 0.1),
        'hgrn_norm_g': 1.0 + nrm(ks[5], (N_HGRN, HGRN_DV), 0.02),
        'hgrn_w_out': nrm(ks[6], (N_HGRN, D, D), D ** -0.5),
        'gmlp_w_in': nrm(ks[7], (N_GMLP, D, 2 * D), D ** -0.5),
        'gmlp_ln_g': 1.0 + nrm(ks[8], (N_GMLP, D), 0.02),
        'gmlp_ln_b': nrm(ks[9], (N_GMLP, D), 0.02),
        'gmlp_w_s': nrm(ks[10], (N_GMLP, GMLP_GROUPS, GMLP_CHUNK, GMLP_CHUNK), GMLP_CHUNK ** -0.5),
        'gmlp_b_s': 1.0 + nrm(ks[11], (N_GMLP, GMLP_GROUPS, GMLP_CHUNK), 0.02),
        'gmlp_w_out': nrm(ks[12], (N_GMLP, D, D), D ** -0.5),
        'ffn_w_up': nrm(ks[13], (DEPTH, D, 2 * D_FF), D ** -0.5),
        'ffn_conv_w': nrm(ks[14], (DEPTH, CONV_W, 2 * D_FF), CONV_W ** -0.5),
        'ffn_conv_b': nrm(ks[15], (DEPTH, 2 * D_FF), 0.01),
        'ffn_w_down': nrm(ks[16], (DEPTH, D_FF, D), D_FF ** -0.5),
        'ple_w_in': nrm(ks[17], (DEPTH, PLE_DIM, D), PLE_DIM ** -0.5),
        'ple_w_gate': nrm(ks[18], (DEPTH, D, D), D ** -0.5),
        'ple_norm_g': 1.0 + nrm(ks[19], (DEPTH, 2, D), 0.02),
    }


def reference(x, p, norm_g, hgrn_w_in, hgrn_lb_logits, hgrn_norm_g, hgrn_w_out,
              gmlp_w_in, gmlp_ln_g, gmlp_ln_b, gmlp_w_s, gmlp_b_s, gmlp_w_out,
              ffn_w_up, ffn_conv_w, ffn_conv_b, ffn_w_down,
              ple_w_in, ple_w_gate, ple_norm_g):
    lb_all = jnp.cumsum(jax.nn.softmax(hgrn_lb_logits.astype(jnp.float32), axis=0), axis=0)
    r = x
    for i in range(DEPTH):
        j = i // N_MIXERS
        hn = rms_norm(r, norm_g[i, 0])
        if i % N_MIXERS == 0:
            m = hgrn2_mixer(hn, hgrn_w_in[j], lb_all[i], hgrn_norm_g[j], hgrn_w_out[j])
        else:
            m = gmlp_mixer(hn, gmlp_w_in[j], gmlp_ln_g[j], gmlp_ln_b[j],
                           gmlp_w_s[j], gmlp_b_s[j], gmlp_w_out[j])
        r = r + rms_norm(m, norm_g[i, 1])
        f = conv_ffn(rms_norm(r, norm_g[i, 2]), ffn_w_up[i], ffn_conv_w[i],
                     ffn_conv_b[i], ffn_w_down[i])
        r = r + rms_norm(f, norm_g[i, 3])
        e = rms_norm(p[i] @ ple_w_in[i], ple_norm_g[i, 0])
        gate = jax.nn.sigmoid(rms_norm(r, ple_norm_g[i, 1]) @ ple_w_gate[i])
        r = r + gate * e
    return r
```

```python
import os
from contextlib import ExitStack
import numpy as np
import concourse.bass as bass
import concourse.mybir as mybir
from concourse.bass_utils import run_bass_kernel_spmd

F32 = mybir.dt.float32
BF16 = mybir.dt.bfloat16
AF = mybir.ActivationFunctionType
ALU = mybir.AluOpType

D = 2048
KC = 16
T = 512
SEQ = 4096
NPRE = int(os.environ.get('MK_NPRE', '3'))
NEXT = int(os.environ.get('MK_NEXT', '5'))
EXT = 5 * T
DFF = 8192
EPS = 1e-6
O_NG = 0
O_PG = 128
O_CW = 192
O_CB = 960
O_LNG = 1216
O_LNB = 1232
O_GNO = 1248
O_HM = 1249
NSP = 1250


class _Op:
    __slots__ = ("fn", "waits", "signal", "dma")

    def __init__(self, fn, dma):
        self.fn = fn
        self.waits = {}
        self.signal = False
        self.dma = dma


class Sched:
    ENGS = ("pe", "act", "dve", "pool", "sp")

    def __init__(self):
        self.ops = {e: [] for e in self.ENGS}
        self.lastw = {}
        self.readers = {}
        self.known = {e: {} for e in self.ENGS}
        self.dma_cnt = {}
        self.floor = {e: set() for e in self.ENGS}

    def add(self, eng, fn, r=(), w=(), dma=None, strict=False):
        op = _Op(fn, dma)
        idx = len(self.ops[eng])
        if dma is not None:
            k = self.dma_cnt.get(dma, 0) + 1
            self.dma_cnt[dma] = k
            token = ("dma", dma, k)
        else:
            token = ("eng", eng, idx)
        deps = set(self.floor[eng])
        self.floor[eng] = set()
        for x in r:
            t = self.lastw.get(x)
            if t is not None:
                deps.add(t)
        for x in w:
            t = self.lastw.get(x)
            if t is not None:
                deps.add(t)
            deps.update(self.readers.get(x, ()))
        kn = self.known[eng]
        for (kind, key, n) in deps:
            if kind == "eng" and key == eng and not strict:
                continue
            if kn.get((kind, key), -1) >= n:
                continue
            kn[(kind, key)] = n
            if op.waits.get((kind, key), -1) < n:
                op.waits[(kind, key)] = n
            if kind == "eng":
                self.ops[key][n].signal = True
        for x in r:
            self.readers.setdefault(x, []).append(token)
        for x in w:
            self.lastw[x] = token
            self.readers[x] = []
        self.ops[eng].append(op)
        return token

    def barrier(self, engs=("pe", "act", "dve")):
        toks = set()
        for e in engs:
            if self.ops[e]:
                toks.add(("eng", e, len(self.ops[e]) - 1))
        for e in engs:
            self.floor[e] |= toks

    def finish(self, eng="sp"):
        op = _Op(None, None)
        for ch, k in self.dma_cnt.items():
            op.waits[("dma", ch)] = k
        self.ops[eng].append(op)

    def emit(self, nc):
        vals = {}
        for e in self.ENGS:
            c = 0
            v = []
            for op in self.ops[e]:
                if op.signal:
                    c += 1
                v.append(c)
            vals[e] = v
        with ExitStack() as es:
            esem = {}
            for e in self.ENGS:
                if any(op.signal for op in self.ops[e]):
                    esem[e] = es.enter_context(nc.semaphore("s_" + e))
            dsem = {ch: es.enter_context(nc.semaphore("d_" + str(ch))) for ch in self.dma_cnt}
            block = es.enter_context(nc.Block())

            def run(e, engine):
                for op in self.ops[e]:
                    for (kind, key), n in op.waits.items():
                        if kind == "eng":
                            engine.wait_ge(esem[key], vals[key][n])
                        else:
                            engine.wait_ge(dsem[key], 16 * n)
                    if op.fn is None:
                        continue
                    ins = op.fn(engine)
                    if op.dma is not None:
                        ins.then_inc(dsem[op.dma], 16)
                    elif op.signal:
                        ins.then_inc(esem[e], 1)

            @block.tensor
            def _(eng):
                run("pe", eng)

            @block.scalar
            def _(eng):
                run("act", eng)

            @block.vector
            def _(eng):
                run("dve", eng)

            @block.gpsimd
            def _(eng):
                run("pool", eng)

            @block.sync
            def _(eng):
                run("sp", eng)


def build(stop=None):
    nc = bass.Bass("TRN2", target_bir_lowering=False)
    S = Sched()
    maxops = int(os.environ.get('MK_MAXOPS', '0'))
    opcnt = [0]

    def A(eng, fn, r=(), w=(), dma=None, strict=False):
        if eng != "sp":
            opcnt[0] += 1
            if maxops and opcnt[0] > maxops:
                return None
        psr_ = [x for x in r if isinstance(x, tuple) and x[0] == "ps"]
        if psr_:
            r = [x for x in r if not (isinstance(x, tuple) and x[0] == "ps")]
            w = list(w) + psr_
        return S.add(eng, fn, r=r, w=w, dma=dma, strict=strict)
    smallset = set(os.environ.get('MK_SMALLW', '').split(','))

    def din(name, shape):
        return nc.dram_tensor(name, shape, F32, kind="ExternalInput").ap()

    xT = din("xT", [128, KC, SEQ])
    pT = din("pT", [2, 128, 2, EXT])
    spd = din("spd", [128, NSP])
    lbl = din("lbl", [3, D])
    wsT = din("wsT", [128, 16, 128])
    bsd = din("bsd", [D])
    hw_in = din("hw_in", [D, 4 * D])
    hw_out = din("hw_out", [1, 1] if "hw_out" in smallset else [D, D])
    gw_in = din("gw_in", [1, 1] if "gw_in" in smallset else [D, 2 * D])
    gw_out = din("gw_out", [1, 1] if "gw_out" in smallset else [D, D])
    w_up = din("w_up", [1, 1] if "w_up" in smallset else [2, D, 2 * DFF])
    w_dn = din("w_dn", [1, 1] if "w_dn" in smallset else [2, DFF, D])
    pw_in = din("pw_in", [1, 1] if "pw_in" in smallset else [2, 256, D])
    pw_g = din("pw_g", [1, 1] if "pw_g" in smallset else [2, D, D])
    outT = nc.dram_tensor("outT", [128, KC, 4 * T], F32, kind="ExternalOutput").ap()

    es = ExitStack()
    uid = [0]

    def un(name):
        uid[0] += 1
        return "%s_%d" % (name, uid[0])

    def sb(name, shape, dt=F32):
        return es.enter_context(nc.sbuf_tensor(name, shape, dt))

    r = sb("r", [128, KC, T])
    hn = sb("hn", [128, KC, T], BF16)
    yb = sb("yb", [128, KC, T], BF16)
    mb = sb("mb", [128, KC, T])
    WS = [sb(f"W{i}", [128, 4096], BF16) for i in range(4)]
    spm = sb("spm", [128, NSP])
    lbc = sb("lbc", [128, D])
    C2 = sb("C2", [128, 16, 128])
    WST = sb("WST", [128, 16, 128], BF16)
    TRI = sb("TRI", [128, 128])
    M2 = sb("M2", [128, 128])
    MNEG = sb("MNEG", [128, 128])
    TRIF = sb("TRIF", [128, 128])
    IND = sb("IND", [128, 32])
    ONES = sb("ONES", [128, 128], BF16)
    IDENT = sb("IDENT", [128, 128], BF16)
    IDF = sb("IDF", [128, 128])
    EPSC = sb("EPSC", [128, 1])
    stf = sb("stf", [128, 16, 128])
    stb = sb("stb", [128, 16, 2, 128], BF16)
    HALO = sb("HALO", [128, 2, 128, 2])
    RS = [sb(f"RS{i}", [128, T]) for i in range(2)]
    SQ = [sb(f"SQ{i}", [128, T], BF16) for i in range(2)]
    pTb = sb("pTb", [128, 2, T], BF16)
    PB = [es.enter_context(nc.psum_tensor(f"PB{i}", [128, 512], F32)) for i in range(7)]
    PH = es.enter_context(nc.psum_tensor("PH", [128, 1024], BF16))

    def pr(bank, c0, c1):
        return [("ps", bank)]

    def prh(par):
        return [("ps", 7)]

    def col(off):
        return spm[:, off:off + 1]

    A("sp", lambda e: e.dma_start(out=spm[:], in_=spd), w=["spm"], dma="c_spm")
    A("pool", lambda e: e.memset(ONES[:], 1.0), w=["ones"])
    A("pool", lambda e: e.memset(EPSC[:], EPS), w=["epsc"])
    A("pool", lambda e: e.memset(IDF[:], 0.0), w=["idf"])
    A("pool", lambda e: e.affine_select(out=IDF[:], in_=IDF[:], pattern=[[-1, 128]], compare_op=ALU.not_equal,
                                        fill=1.0, base=0, channel_multiplier=1), r=["idf"], w=["idf"])
    A("dve", lambda e: e.tensor_copy(out=IDENT[:], in_=IDF[:]), r=["idf"], w=["ident"])
    A("pool", lambda e: e.memset(TRIF[:], 1.0), w=["trif"])
    A("pool", lambda e: e.affine_select(out=TRIF[:], in_=TRIF[:], pattern=[[1, 128]], compare_op=ALU.is_ge,
                                        fill=0.0, base=0, channel_multiplier=-1), r=["trif"], w=["trif"])
    A("pool", lambda e: e.tensor_copy(out=TRI[:], in_=TRIF[:]), r=["trif"], w=["tri"])
    A("pool", lambda e: e.memset(TRI[0:64, 64:128], 0.0), r=["tri"], w=["tri"])
    A("pool", lambda e: e.memset(M2[:], 1.0), w=["m2"])
    A("pool", lambda e: e.affine_select(out=M2[:], in_=M2[:], pattern=[[-1, 128]], compare_op=ALU.is_gt,
                                        fill=0.0, base=0, channel_multiplier=1), r=["m2"], w=["m2"])
    A("pool", lambda e: e.memset(M2[64:128, 0:64], 0.0), r=["m2"], w=["m2"])
    A("dve", lambda e: e.tensor_scalar_mul(out=MNEG[:], in0=TRI[:], scalar1=-1.0), r=["tri"], w=["mneg"])
    A("pool", lambda e: e.memset(IND[:], 0.0), w=["ind"])
    A("pool", lambda e: e.memset(IND[0:64, 0:1], 1.0), r=["ind"], w=["ind"])
    A("pool", lambda e: e.memset(IND[64:128, 1:2], 1.0), r=["ind"], w=["ind"])
    A("pool", lambda e: e.memset(stf[:], 0.0), w=["stf%d" % h for h in range(16)])
    A("pool", lambda e: e.memset(stb[:], 0.0), w=["stb%d_%d" % (h, c) for h in range(16) for c in range(2)])
    A("pool", lambda e: e.memset(HALO[:], 0.0), w=["halo0", "halo1"])

    mbv = mb[:].rearrange("p a b -> p (a b)")
    MBALL = ["mb%d" % k for k in range(16)]
    for i in range(3):
        A("sp", (lambda i: lambda e: e.dma_start(out=mbv[:, i * D:(i + 1) * D], in_=lbl[i].partition_broadcast(128)))(i),
          w=MBALL[4 * i:4 * i + 4], dma="c_lb%d" % i)
        A("act", (lambda i: lambda e: e.activation(out=mbv[:, i * D:(i + 1) * D], in_=mbv[:, i * D:(i + 1) * D], func=AF.Exp))(i),
          r=MBALL[4 * i:4 * i + 4], w=MBALL[4 * i:4 * i + 4])
    A("dve", lambda e: e.tensor_tensor(out=mbv[:, D:2 * D], in0=mbv[:, D:2 * D], in1=mbv[:, 2 * D:3 * D], op=ALU.add),
      r=MBALL[4:12], w=MBALL[4:8])
    A("dve", lambda e: e.tensor_tensor(out=mbv[:, D:2 * D], in0=mbv[:, D:2 * D], in1=mbv[:, 0:D], op=ALU.add),
      r=MBALL[0:8], w=MBALL[4:8])
    A("dve", lambda e: e.reciprocal(out=mbv[:, D:2 * D], in_=mbv[:, D:2 * D]), r=MBALL[4:8], w=MBALL[4:8])
    A("dve", lambda e: e.tensor_tensor(out=lbc[:], in0=mbv[:, 0:D], in1=mbv[:, D:2 * D], op=ALU.mult),
      r=MBALL[0:8], w=["lbc"])
    wsf = mbv[:, 3 * D:4 * D].rearrange("p (g t) -> p g t", g=16)
    A("sp", lambda e: e.dma_start(out=wsf, in_=wsT), w=MBALL[12:16], dma="c_ws")
    A("sp", lambda e: e.dma_start(out=C2[:].rearrange("p g t -> p (g t)"), in_=bsd.partition_broadcast(128)),
      w=["c2"], dma="c_bs")
    for g in range(16):
        A("dve", (lambda g: lambda e: e.tensor_tensor(out=WST[:, g, :], in0=wsf[:, g, :], in1=TRIF[:], op=ALU.mult))(g),
          r=MBALL[12:16] + ["trif"], w=["wst"])
    for g in range(16):
        bank = 1 + (g % 2)
        A("pe", (lambda g, bank: lambda e: e.matmul(PB[bank][:, 0:128], lhsT=ONES[:], rhs=WST[:, g, :], start=True, stop=True))(g, bank),
          r=["ones", "wst"], w=pr(bank, 0, 128))
        A("dve", (lambda g, bank: lambda e: e.scalar_tensor_tensor(out=C2[:, g, :], in0=PB[bank][:, 0:128], scalar=col(O_LNB + g),
                                                                   in1=C2[:, g, :], op0=ALU.mult, op1=ALU.add))(g, bank),
          r=pr(bank, 0, 128) + ["spm", "c2"], w=["c2"])
    S.barrier()

    wctr = [0]

    def wload(src_ap, view):
        i = wctr[0] % 4
        wctr[0] += 1
        dst = view(WS[i][:])
        names = ["W%da" % i, "W%db" % i]
        if len(src_ap.shape) == 4:
            for sg_ in range(src_ap.shape[2]):
                A("pool", (lambda d_, s_: lambda e: e.dma_start(out=d_, in_=s_))(dst[:, :, sg_, :], src_ap[:, :, sg_, :]),
                  w=[names[sg_]], dma="w%d" % i)
        else:
            A("pool", lambda e: e.dma_start(out=dst, in_=src_ap), w=names, dma="w%d" % i)
        return WS[i], names

    def v16(ap):
        return ap.rearrange("p (k n) -> p k n", k=16)

    def rms_stats(src, c0, c1, ri, scale):
        N = c1 - c0
        for kc in range(KC):
            ap, rn = src(kc)
            sq = SQ[kc % 2]
            A("act", (lambda ap, sq: lambda e: e.activation(out=sq[:, 0:N], in_=ap, func=AF.Square))(ap, sq),
              r=[rn], w=["sq%d" % (kc % 2)])
            A("pe", (lambda sq, kc: lambda e: e.matmul(PB[0][:, 0:N], lhsT=ONES[:], rhs=sq[:, 0:N], start=(kc == 0), stop=(kc == KC - 1)))(sq, kc),
              r=["sq%d" % (kc % 2), "ones"], w=pr(0, 0, N))
        A("act", lambda e: e.activation(out=RS[ri][:, 0:N], in_=PB[0][:, 0:N], func=AF.Ln, scale=scale, bias=EPSC[:]),
          r=pr(0, 0, N) + ["epsc"], w=["rs%d" % ri])
        A("act", lambda e: e.activation(out=RS[ri][:, 0:N], in_=RS[ri][:, 0:N], func=AF.Exp, scale=-0.5),
          r=["rs%d" % ri], w=["rs%d" % ri])

    def pre_norm(goff, c0, c1):
        N = c1 - c0
        rms_stats(lambda kc: (r[:, kc, c0:c1], "r%d" % kc), c0, c1, 0, 1.0 / D)
        for kc in range(KC):
            A("dve", (lambda kc: lambda e: e.scalar_tensor_tensor(out=hn[:, kc, c0:c1], in0=r[:, kc, c0:c1], scalar=col(goff + kc),
                                                                  in1=RS[0][:, 0:N], op0=ALU.mult, op1=ALU.mult))(kc),
              r=["r%d" % kc, "rs0", "spm"], w=["hn%d" % kc])

    def post_norm_add(goff, c0, c1):
        N = c1 - c0
        rms_stats(lambda kc: (mb[:, kc, c0:c1], "mb%d" % kc), c0, c1, 1, 1.0 / D)
        for kc in range(KC):
            A("dve", (lambda kc: lambda e: e.scalar_tensor_tensor(out=mb[:, kc, c0:c1], in0=mb[:, kc, c0:c1], scalar=col(goff + kc),
                                                                  in1=RS[1][:, 0:N], op0=ALU.mult, op1=ALU.mult))(kc),
              r=["mb%d" % kc, "rs1", "spm"], w=["mb%d" % kc])
            A("dve", (lambda kc: lambda e: e.tensor_tensor(out=r[:, kc, c0:c1], in0=r[:, kc, c0:c1], in1=mb[:, kc, c0:c1], op=ALU.add))(kc),
              r=["mb%d" % kc, "r%d" % kc], w=["r%d" % kc])

    pbank = [0]

    def proj_fm(wsrc, src, c0, c1, evac):
        N = c1 - c0
        for q in range(8):
            wt, wr = wload(wsrc.rearrange("(k p) n -> p k n", p=128)[:, :, q * 256:(q + 1) * 256], v16)
            w3 = v16(wt[:])
            for m2 in range(2):
                mc = q * 2 + m2
                bank = 2 + (pbank[0] % 4)
                pbank[0] += 1
                for kc in range(KC):
                    ap, rn = src(kc)
                    A("pe", (lambda bank, w3, kc, m2, ap: lambda e: e.matmul(PB[bank][:, 0:N], lhsT=w3[:, kc, m2 * 128:(m2 + 1) * 128], rhs=ap,
                                                                             start=(kc == 0), stop=(kc == KC - 1)))(bank, w3, kc, m2, ap),
                      r=wr + [rn], w=pr(bank, 0, N))
                evac(mc, PB[bank][:, 0:N], pr(bank, 0, N))

    def evac_to_mb(c0, c1):
        def f(mc, ps, psr):
            A("act", lambda e: e.activation(out=mb[:, mc, c0:c1], in_=ps, func=AF.Copy), r=psr, w=["mb%d" % mc])
        return f

    def hgrn(state_only):
        with ExitStack() as hs:
            def tb(name, shape, dt=F32):
                return hs.enter_context(nc.sbuf_tensor(un(name), shape, dt))
            GS = tb("h_gs", [128, T])
            tf = {}
            for par in range(2):
                for nm in ("sg", "ns", "lg", "eb", "enb", "ed", "y1", "ln"):
                    tf[nm, par] = tb(f"h_{nm}{par}", [128, 128])
                for nm in ("q", "k", "kd", "v0", "v1", "qT", "kT", "sc", "sq"):
                    tf[nm, par] = tb(f"h_{nm}{par}", [128, 128], BF16)
                tf["ebl", par] = tb(f"h_ebl{par}", [128, 2])
            win = hw_in.rearrange("(k p) (s n) -> p k s n", p=128, s=4)
            def head(h):
                hc = slice(h * 128, (h + 1) * 128)
                wB3 = wBr = None
                if state_only:
                    wA, wAr = wload(win[:, :, 1:3, hc], lambda a: a.rearrange("p (k s n) -> p k s n", k=16, s=2))
                    wA3 = v16(wA[:])
                else:
                    wA, wAr = wload(win[:, :, 0:2, hc], lambda a: a.rearrange("p (k s n) -> p k s n", k=16, s=2))
                    wB, wBr = wload(win[:, :, 2:4, hc], lambda a: a.rearrange("p (k s n) -> p k s n", k=16, s=2))
                    wA3 = v16(wA[:])
                    wB3 = v16(wB[:])
                    for kc in range(KC):
                        A("pe", (lambda kc, wB3: lambda e: e.matmul(PB[4][:, :], lhsT=wB3[:, kc, 128:256], rhs=hn[:, kc, :],
                                                                    start=(kc == 0), stop=(kc == KC - 1)))(kc, wB3),
                          r=wBr + ["hn%d" % kc], w=pr(4, 0, 512))
                    A("act", lambda e: e.activation(out=GS[:], in_=PB[4][:, :], func=AF.Silu), r=pr(4, 0, 512), w=["h_gs"])
                for blk in range(4):
                    hblock(h, blk, hc, wA3, wAr, wB3, wBr)

            def hblock(h, blk, hc, wA3, wAr, wB3, wBr):
                if True:
                    par = blk % 2
                    bc = slice(blk * 128, (blk + 1) * 128)
                    PJ = PB[2 + par]
                    t = lambda nm: tf[nm, par]
                    R = lambda nm: "h_%s%d" % (nm, par)
                    if state_only:
                        for kc in range(KC):
                            A("pe", (lambda kc, PJ, wA3, bc: lambda e: e.matmul(PJ[:, 128:384], lhsT=hn[:, kc, bc], rhs=wA3[:, kc, :],
                                                                                start=(kc == 0), stop=(kc == KC - 1)))(kc, PJ, wA3, bc),
                              r=wAr + ["hn%d" % kc], w=pr(2 + par, 128, 384))
                    else:
                        for kc in range(KC):
                            A("pe", (lambda kc, PJ, wA3, bc: lambda e: e.matmul(PJ[:, 0:256], lhsT=hn[:, kc, bc], rhs=wA3[:, kc, :],
                                                                                start=(kc == 0), stop=(kc == KC - 1)))(kc, PJ, wA3, bc),
                              r=wAr + ["hn%d" % kc], w=pr(2 + par, 0, 256))
                        for kc in range(KC):
                            A("pe", (lambda kc, PJ, wB3, bc: lambda e: e.matmul(PJ[:, 256:384], lhsT=hn[:, kc, bc], rhs=wB3[:, kc, 0:128],
                                                                                start=(kc == 0), stop=(kc == KC - 1)))(kc, PJ, wB3, bc),
                              r=wBr + ["hn%d" % kc], w=pr(2 + par, 256, 384))
                    sg, ns, lg = t("sg"), t("ns"), t("lg")
                    A("act", (lambda sg, PJ: lambda e: e.activation(out=sg[:], in_=PJ[:, 128:256], func=AF.Sigmoid))(sg, PJ),
                      r=pr(2 + par, 128, 256), w=[R("sg")])
                    A("act", (lambda ns, PJ: lambda e: e.activation(out=ns[:], in_=PJ[:, 128:256], func=AF.Sigmoid, scale=-1.0))(ns, PJ),
                      r=pr(2 + par, 128, 256), w=[R("ns")])
                    A("dve", (lambda ns, hc: lambda e: e.tensor_tensor(out=ns[:], in0=ns[:], in1=lbc[:, hc], op=ALU.mult))(ns, hc),
                      r=[R("ns"), "lbc"], w=[R("ns")])
                    A("dve", (lambda sg, ns: lambda e: e.tensor_tensor(out=sg[:], in0=sg[:], in1=ns[:], op=ALU.add))(sg, ns),
                      r=[R("ns"), R("sg")], w=[R("sg")])
                    A("act", (lambda lg, sg: lambda e: e.activation(out=lg[:], in_=sg[:], func=AF.Ln))(lg, sg),
                      r=[R("sg")], w=[R("lg")])
                    BDc = par * 256
                    if not state_only:
                        A("pe", (lambda lg, BDc: lambda e: e.matmul(PB[5][:, BDc:BDc + 128], lhsT=TRI[:], rhs=lg[:], start=True, stop=True))(lg, BDc),
                          r=[R("lg"), "tri"], w=pr(5, BDc, BDc + 128))
                    A("pe", (lambda lg, BDc: lambda e: e.matmul(PB[5][:, BDc + 128:BDc + 256], lhsT=M2[:], rhs=lg[:], start=True, stop=True))(lg, BDc),
                      r=[R("lg"), "m2"], w=pr(5, BDc + 128, BDc + 256))
                    EBc = 256 + par * 128
                    A("pe", (lambda lg, EBc: lambda e: e.matmul(PB[0][:, EBc:EBc + 32], lhsT=lg[:], rhs=IND[:], start=True, stop=True))(lg, EBc),
                      r=[R("lg"), "ind"], w=pr(0, EBc, EBc + 32))
                    ed, ebl, kd = t("ed"), t("ebl"), t("kd")
                    vm = [t("v0"), t("v1")]
                    A("act", (lambda ed, BDc: lambda e: e.activation(out=ed[:], in_=PB[5][:, BDc + 128:BDc + 256], func=AF.Exp))(ed, BDc),
                      r=pr(5, BDc + 128, BDc + 256), w=[R("ed")])
                    A("act", (lambda ebl, EBc: lambda e: e.activation(out=ebl[:], in_=PB[0][:, EBc:EBc + 2], func=AF.Exp))(ebl, EBc),
                      r=pr(0, EBc, EBc + 2), w=[R("ebl")])
                    A("dve", (lambda kd, sg, ed: lambda e: e.scalar_tensor_tensor(out=kd[:], in0=sg[:], scalar=-1.0, in1=ed[:],
                                                                                  op0=ALU.add, op1=ALU.mult))(kd, sg, ed),
                      r=[R("sg"), R("ed")], w=[R("kd")])
                    for c in range(2):
                        A("act", (lambda vv, PJ, c: lambda e: e.activation(out=vv[:], in_=PJ[:, 256:384], func=AF.Copy, scale=IND[:, c:c + 1]))(vm[c], PJ, c),
                          r=pr(2 + par, 256, 384) + ["ind"], w=[R("v%d" % c)])
                    if not state_only:
                        eb, enb, q, k, qT, kT, sc, sq, y1, ln = (t(n) for n in ("eb", "enb", "q", "k", "qT", "kT", "sc", "sq", "y1", "ln"))
                        A("act", (lambda eb, BDc: lambda e: e.activation(out=eb[:], in_=PB[5][:, BDc:BDc + 128], func=AF.Exp))(eb, BDc),
                          r=pr(5, BDc, BDc + 128), w=[R("eb")])
                        A("act", (lambda enb, BDc: lambda e: e.activation(out=enb[:], in_=PB[5][:, BDc:BDc + 128], func=AF.Exp, scale=-1.0))(enb, BDc),
                          r=pr(5, BDc, BDc + 128), w=[R("enb")])
                        A("dve", (lambda q, PJ, eb: lambda e: e.tensor_tensor(out=q[:], in0=PJ[:, 0:128], in1=eb[:], op=ALU.mult))(q, PJ, eb),
                          r=pr(2 + par, 0, 128) + [R("eb")], w=[R("q")])
                        A("dve", (lambda k, sg, enb: lambda e: e.scalar_tensor_tensor(out=k[:], in0=sg[:], scalar=-1.0, in1=enb[:],
                                                                                      op0=ALU.add, op1=ALU.mult))(k, sg, enb),
                          r=[R("sg"), R("enb")], w=[R("k")])
                        A("pe", (lambda q: lambda e: e.transpose(PH[:, par * 256:par * 256 + 128], q[:], IDENT[:]))(q),
                          r=[R("q"), "ident"], w=prh(par))
                        A("pe", (lambda k: lambda e: e.transpose(PH[:, par * 256 + 128:par * 256 + 256], k[:], IDENT[:]))(k),
                          r=[R("k"), "ident"], w=prh(par))
                        A("dve", (lambda qT: lambda e: e.tensor_copy(out=qT[:], in_=PH[:, par * 256:par * 256 + 128]))(qT),
                          r=prh(par), w=[R("qT")])
                        A("act", (lambda kT: lambda e: e.activation(out=kT[:], in_=PH[:, par * 256 + 128:par * 256 + 256], func=AF.Copy))(kT),
                          r=prh(par), w=[R("kT")])
                        SCc = par * 256
                        A("pe", (lambda kT, qT, SCc: lambda e: e.matmul(PB[6][:, SCc:SCc + 128], lhsT=kT[:], rhs=qT[:], start=True, stop=True))(kT, qT, SCc),
                          r=[R("kT"), R("qT")], w=pr(6, SCc, SCc + 128))
                        A("dve", (lambda sc, SCc: lambda e: e.tensor_tensor(out=sc[:], in0=PB[6][:, SCc:SCc + 128], in1=MNEG[:], op=ALU.mult))(sc, SCc),
                          r=pr(6, SCc, SCc + 128) + ["mneg"], w=[R("sc")])
                    OTc = par * 256 + 128
                    for c in range(2):
                        cs = slice(c * 64, (c + 1) * 64)
                        KVc = c * 128
                        if not state_only:
                            A("pe", (lambda v, sc, cs, OTc, c: lambda e: e.matmul(PB[6][:, OTc + c * 64:OTc + c * 64 + 64], lhsT=v[:], rhs=sc[:, cs],
                                                                                  start=True, stop=False))(vm[c], sc, cs, OTc, c),
                              r=[R("v%d" % c), R("sc")], w=pr(6, OTc, OTc + 128))
                            A("pe", (lambda qT, cs, OTc, c: lambda e: e.matmul(PB[6][:, OTc + c * 64:OTc + c * 64 + 64], lhsT=stb[:, h, c, :], rhs=qT[:, cs],
                                                                               start=False, stop=True))(qT, cs, OTc, c),
                              r=[R("qT"), "stb%d_%d" % (h, c)], w=pr(6, OTc, OTc + 128))
                        A("pe", (lambda kd, v, cs, KVc: lambda e: e.matmul(PB[1][:, KVc:KVc + 128], lhsT=kd[:], rhs=v[:], start=True, stop=True))(kd, vm[c], cs, KVc),
                          r=[R("kd"), R("v%d" % c)], w=pr(1, KVc, KVc + 128))
                        A("dve", (lambda ebl, c, KVc: lambda e: e.scalar_tensor_tensor(out=stf[:, h, :], in0=stf[:, h, :], scalar=ebl[:, c:c + 1],
                                                                                       in1=PB[1][:, KVc:KVc + 128], op0=ALU.mult, op1=ALU.subtract))(ebl, c, KVc),
                          r=pr(1, KVc, KVc + 128) + [R("ebl"), "stf%d" % h], w=["stf%d" % h])
                        nxt = (c + 1) % 2
                        A("act", (lambda nxt: lambda e: e.activation(out=stb[:, h, nxt, :], in_=stf[:, h, :], func=AF.Copy))(nxt),
                          r=["stf%d" % h], w=["stb%d_%d" % (h, nxt)])
                    if not state_only:
                        A("act", (lambda sq, OTc: lambda e: e.activation(out=sq[:], in_=PB[6][:, OTc:OTc + 128], func=AF.Square))(sq, OTc),
                          r=pr(6, OTc, OTc + 128), w=[R("sq")])
                        SSc = par * 128
                        A("pe", (lambda sq, SSc: lambda e: e.matmul(PB[0][:, SSc:SSc + 128], lhsT=ONES[:], rhs=sq[:], start=True, stop=True))(sq, SSc),
                          r=[R("sq"), "ones"], w=pr(0, SSc, SSc + 128))
                        A("act", (lambda ln, SSc: lambda e: e.activation(out=ln[:], in_=PB[0][:, SSc:SSc + 128], func=AF.Ln, scale=1.0 / 128, bias=EPSC[:]))(ln, SSc),
                          r=pr(0, SSc, SSc + 128) + ["epsc"], w=[R("ln")])
                        A("act", (lambda ln: lambda e: e.activation(out=ln[:], in_=ln[:], func=AF.Exp, scale=-0.5))(ln),
                          r=[R("ln")], w=[R("ln")])
                        A("dve", (lambda y1, ln, OTc: lambda e: e.scalar_tensor_tensor(out=y1[:], in0=PB[6][:, OTc:OTc + 128], scalar=col(O_GNO), in1=ln[:],
                                                                                       op0=ALU.mult, op1=ALU.mult))(y1, ln, OTc),
                          r=pr(6, OTc, OTc + 128) + [R("ln"), "spm"], w=[R("y1")])
                        A("dve", (lambda y1, bc: lambda e: e.tensor_tensor(out=yb[:, h, bc], in0=y1[:], in1=GS[:, bc], op=ALU.mult))(y1, bc),
                          r=[R("y1"), "h_gs"], w=["yb%d" % h])

            for h in range(int(os.environ.get('MK_HEADS', '16'))):
                head(h)
        S.barrier()

    def gmlp(c0, c1):
        N = c1 - c0
        b0, b1 = c0 // 128, c1 // 128
        with ExitStack() as hs:
            def tb(name, shape, dt=F32):
                return hs.enter_context(nc.sbuf_tensor(un(name), shape, dt))
            vln = tb("g_vln", [128, 4, D], BF16)
            tu = [tb(f"g_u{i}", [128, T]) for i in range(2)]
            tm = [tb(f"g_m{i}", [128, T]) for i in range(2)]
            st = tb("g_st", [128, 4, 4])
            mv = tb("g_mv", [128, 4, 2])
            rsd = tb("g_rsd", [128, 4])
            win = gw_in.rearrange("(k p) n -> p k n", p=128)
            vf = mb[:].rearrange("p a b -> p (a b)")
            for q8 in range(8):
                wt, wr = wload(win[:, :, D + q8 * 256:D + (q8 + 1) * 256], v16)
                w3 = v16(wt[:])
                for blk in range(b0, b1):
                    bank = 2 + (pbank[0] % 4)
                    pbank[0] += 1
                    bc = slice(blk * 128, (blk + 1) * 128)
                    for kc in range(KC):
                        A("pe", (lambda bank, kc, bc, w3: lambda e: e.matmul(PB[bank][:, 0:256], lhsT=hn[:, kc, bc], rhs=w3[:, kc, :],
                                                                             start=(kc == 0), stop=(kc == KC - 1)))(bank, kc, bc, w3),
                          r=wr + ["hn%d" % kc], w=pr(bank, 0, 256))
                    off = blk * D + q8 * 256
                    A("act", (lambda bank, off: lambda e: e.activation(out=vf[:, off:off + 256], in_=PB[bank][:, 0:256], func=AF.Gelu))(bank, off),
                      r=pr(bank, 0, 256), w=["mb%d" % (off // 512)])
            for blk in range(b0, b1):
                mbr = ["mb%d" % (blk * 4 + i) for i in range(4)]
                A("dve", (lambda blk: lambda e: e.memset(st[:, blk, :], 0.0))(blk), w=["g_st%d" % blk])
                A("act", (lambda blk: lambda e: e.activation(out=vln[:, blk, :], in_=vf[:, blk * D:(blk + 1) * D], func=AF.Copy,
                                                             accum_out=st[:, blk, 0:1]))(blk),
                  r=mbr + ["g_st%d" % blk], w=["g_vln%d" % blk, "g_st%d" % blk])
                A("act", (lambda blk: lambda e: e.activation(out=vln[:, blk, :], in_=vf[:, blk * D:(blk + 1) * D], func=AF.Square,
                                                             accum_out=st[:, blk, 1:2]))(blk),
                  r=mbr + ["g_st%d" % blk], w=["g_vln%d" % blk, "g_st%d" % blk])
                A("dve", (lambda blk: lambda e: e.tensor_scalar_mul(out=mv[:, blk, 0:1], in0=st[:, blk, 0:1], scalar1=1.0 / D))(blk),
                  r=["g_st%d" % blk], w=["g_mv%d" % blk])
                A("dve", (lambda blk: lambda e: e.tensor_tensor(out=st[:, blk, 2:3], in0=mv[:, blk, 0:1], in1=mv[:, blk, 0:1], op=ALU.mult))(blk),
                  r=["g_mv%d" % blk], w=["g_st%d" % blk], strict=True)
                A("dve", (lambda blk: lambda e: e.scalar_tensor_tensor(out=mv[:, blk, 1:2], in0=st[:, blk, 1:2], scalar=1.0 / D, in1=st[:, blk, 2:3],
                                                                       op0=ALU.mult, op1=ALU.subtract))(blk),
                  r=["g_st%d" % blk], w=["g_mv%d" % blk], strict=True)
                A("act", (lambda blk: lambda e: e.activation(out=rsd[:, blk:blk + 1], in_=mv[:, blk, 1:2], func=AF.Ln, bias=EPSC[:]))(blk),
                  r=["g_mv%d" % blk, "epsc"], w=["g_rsd%d" % blk])
                A("act", (lambda blk: lambda e: e.activation(out=rsd[:, blk:blk + 1], in_=rsd[:, blk:blk + 1], func=AF.Exp, scale=-0.5))(blk),
                  r=["g_rsd%d" % blk], w=["g_rsd%d" % blk], strict=True)
                A("dve", (lambda blk: lambda e: e.tensor_scalar(out=vln[:, blk, :], in0=vf[:, blk * D:(blk + 1) * D], scalar1=mv[:, blk, 0:1],
                                                                scalar2=rsd[:, blk:blk + 1], op0=ALU.subtract, op1=ALU.mult))(blk),
                  r=mbr + ["g_mv%d" % blk, "g_rsd%d" % blk], w=["g_vln%d" % blk], strict=True)
                if os.environ.get('MK_DUMPY') == '2' and blk == 0:
                    A("dve", (lambda blk: lambda e: e.tensor_copy(out=vf[:, blk * D:(blk + 1) * D], in_=vln[:, blk, :]))(blk),
                      r=["g_vln%d" % blk], w=mbr)
                    A("dve", (lambda blk: lambda e: e.tensor_copy(out=vf[:, (blk + 1) * D:(blk + 1) * D + 2], in_=mv[:, blk, :]))(blk),
                      r=["g_mv%d" % blk], w=["mb%d" % ((blk + 1) * 4)])
                    A("dve", (lambda blk: lambda e: e.tensor_copy(out=vf[:, (blk + 1) * D + 2:(blk + 1) * D + 3], in_=rsd[:, blk:blk + 1]))(blk),
                      r=["g_rsd%d" % blk], w=["mb%d" % ((blk + 1) * 4)])
            for g in range(16):
                if g % 2 == 0:
                    wt, wr = wload(win[:, :, (g // 2) * 256:(g // 2 + 1) * 256], v16)
                    w3 = v16(wt[:])
                ub = 4 + (g % 2)
                for kc in range(KC):
                    A("pe", (lambda ub, kc, w3, g: lambda e: e.matmul(PB[ub][:, 0:N], lhsT=w3[:, kc, (g % 2) * 128:(g % 2 + 1) * 128], rhs=hn[:, kc, c0:c1],
                                                                      start=(kc == 0), stop=(kc == KC - 1)))(ub, kc, w3, g),
                      r=wr + ["hn%d" % kc], w=pr(ub, 0, N))
                u = tu[g % 2]
                A("act", (lambda u, ub: lambda e: e.activation(out=u[:, 0:N], in_=PB[ub][:, 0:N], func=AF.Gelu))(u, ub),
                  r=pr(ub, 0, N), w=["g_u%d" % (g % 2)])
                sbk = 6 if g % 2 == 0 else 1
                m = tm[g % 2]
                for blk in range(b0, b1):
                    o = (blk - b0) * 128
                    A("pe", (lambda sbk, o, blk, g: lambda e: e.matmul(PB[sbk][:, o:o + 128], lhsT=vln[:, blk, g * 128:(g + 1) * 128], rhs=WST[:, g, :],
                                                                       start=True, stop=True))(sbk, o, blk, g),
                      r=["g_vln%d" % blk, "wst"], w=pr(sbk, o, o + 128))
                    A("dve", (lambda sbk, o, m, g: lambda e: e.scalar_tensor_tensor(out=m[:, o:o + 128], in0=PB[sbk][:, o:o + 128], scalar=col(O_LNG + g),
                                                                                    in1=C2[:, g, :], op0=ALU.mult, op1=ALU.add))(sbk, o, m, g),
                      r=pr(sbk, o, o + 128) + ["c2", "spm"], w=["g_m%d" % (g % 2)])
                A("dve", (lambda m, u, g: lambda e: e.tensor_tensor(out=yb[:, g, c0:c1], in0=m[:, 0:N], in1=u[:, 0:N], op=ALU.mult))(m, u, g),
                  r=["g_m%d" % (g % 2), "g_u%d" % (g % 2)], w=["yb%d" % g])
        S.barrier()

    def ffn(l, c0, c1, up_only=False):
        N = c1 - c0
        with ExitStack() as hs:
            def tb(name, shape, dt=F32):
                return hs.enter_context(nc.sbuf_tensor(un(name), shape, dt))
            HS = {(z, p): tb(f"f_hs{z}{p}", [128, 2 + T]) for z in range(2) for p in range(2)}
            TA = {(z, p): tb(f"f_ta{z}{p}", [128, T]) for z in range(2) for p in range(2)}
            ACTB = [tb(f"f_act{p}", [128, 2, T], BF16) for p in range(2)]
            wu = w_up[l].rearrange("(k p) (s n) -> p k s n", p=128, s=2)
            wd = w_dn[l].rearrange("(k p) n -> p k n", p=128)
            for ci in range(64):
                par = ci % 2
                rd = ci // 2
                wt, wr = wload(wu[:, :, :, ci * 128:(ci + 1) * 128], lambda a: a.rearrange("p (k s n) -> p k s n", k=16, s=2))
                w3 = v16(wt[:])
                if ci % 2 == 0 and not up_only:
                    wdt, wdr = wload(wd[:, rd * 2:rd * 2 + 2, :], lambda a: a.rearrange("p (k n) -> p k n", k=2))
                    wd3 = wdt[:].rearrange("p (k n) -> p k n", k=2)
                for z in range(2):
                    hci = z * 64 + ci
                    bank = 2 + z * 2 + par
                    for kc in range(KC):
                        A("pe", (lambda bank, kc, w3, z: lambda e: e.matmul(PB[bank][:, 0:N], lhsT=w3[:, kc, z * 128:(z + 1) * 128], rhs=hn[:, kc, c0:c1],
                                                                            start=(kc == 0), stop=(kc == KC - 1)))(bank, kc, w3, z),
                          r=wr + ["hn%d" % kc], w=pr(bank, 0, N))
                    hsb = HS[z, par]
                    ta = TA[z, par]
                    hr = "f_hs%d%d" % (z, par)
                    tr = "f_ta%d%d" % (z, par)
                    A("act", (lambda hsb, hci: lambda e: e.activation(out=hsb[:, 0:2], in_=HALO[:, l, hci, :], func=AF.Copy))(hsb, hci),
                      r=["halo%d" % l], w=[hr])
                    A("act", (lambda hsb, bank: lambda e: e.activation(out=hsb[:, 2:2 + N], in_=PB[bank][:, 0:N], func=AF.Copy))(hsb, bank),
                      r=pr(bank, 0, N), w=[hr])
                    A("act", (lambda hsb, hci: lambda e: e.activation(out=HALO[:, l, hci, :], in_=hsb[:, N:N + 2], func=AF.Copy))(hsb, hci),
                      r=[hr], w=["halo%d" % l])
                    if up_only:
                        continue
                    cw = O_CW + l * 384 + hci
                    A("act", (lambda ta, bank, cw, hci: lambda e: e.activation(out=ta[:, 0:N], in_=PB[bank][:, 0:N], func=AF.Identity,
                                                                               scale=col(cw + 256), bias=col(O_CB + l * 128 + hci)))(ta, bank, cw, hci),
                      r=pr(bank, 0, N) + ["spm"], w=[tr])
                    A("dve", (lambda ta, hsb, cw: lambda e: e.scalar_tensor_tensor(out=ta[:, 0:N], in0=hsb[:, 1:1 + N], scalar=col(cw + 128), in1=ta[:, 0:N],
                                                                                   op0=ALU.mult, op1=ALU.add))(ta, hsb, cw),
                      r=[hr, tr, "spm"], w=[tr])
                    A("dve", (lambda ta, hsb, cw: lambda e: e.scalar_tensor_tensor(out=ta[:, 0:N], in0=hsb[:, 0:N], scalar=col(cw), in1=ta[:, 0:N],
                                                                                   op0=ALU.mult, op1=ALU.add))(ta, hsb, cw),
                      r=[hr, tr, "spm"], w=[tr])
                if up_only:
                    continue
                tg, tv = TA[0, par], TA[1, par]
                A("act", (lambda tg: lambda e: e.activation(out=tg[:, 0:N], in_=tg[:, 0:N], func=AF.Gelu_apprx_tanh))(tg),
                  r=["f_ta0%d" % par], w=["f_ta0%d" % par])
                ab = ACTB[rd % 2]
                A("dve", (lambda ab, tg, tv, par: lambda e: e.tensor_tensor(out=ab[:, par, 0:N], in0=tg[:, 0:N], in1=tv[:, 0:N], op=ALU.mult))(ab, tg, tv, par),
                  r=["f_ta0%d" % par, "f_ta1%d" % par], w=["f_act%d" % (rd % 2)])
                if ci % 2 == 1:
                    for mc in range(KC):
                        bank = 6 if mc % 2 == 0 else 1
                        for a in range(2):
                            A("pe", (lambda bank, a, mc, wd3, ab: lambda e: e.matmul(PB[bank][:, 0:N], lhsT=wd3[:, a, mc * 128:(mc + 1) * 128], rhs=ab[:, a, 0:N],
                                                                                     start=(a == 0), stop=(a == 1)))(bank, a, mc, wd3, ab),
                              r=wdr + ["f_act%d" % (rd % 2)], w=pr(bank, 0, N))
                        if rd == 0:
                            A("act", (lambda bank, mc: lambda e: e.activation(out=mb[:, mc, c0:c1], in_=PB[bank][:, 0:N], func=AF.Copy))(bank, mc),
                              r=pr(bank, 0, N), w=["mb%d" % mc])
                        else:
                            A("dve", (lambda bank, mc: lambda e: e.tensor_tensor(out=mb[:, mc, c0:c1], in0=mb[:, mc, c0:c1], in1=PB[bank][:, 0:N], op=ALU.add))(bank, mc),
                              r=pr(bank, 0, N) + ["mb%d" % mc], w=["mb%d" % mc])
        S.barrier()

    def ple(l, c0, c1, tok0):
        N = c1 - c0
        with ExitStack() as hs:
            sgt = [hs.enter_context(nc.sbuf_tensor(un(f"p_sg{i}"), [128, T], F32)) for i in range(2)]
            A("pool", lambda e: e.dma_start(out=pTb[:, :, c0:c1], in_=pT[l, :, :, tok0 + c0:tok0 + c1]), w=["ptb"], dma="ptb")
            wi = pw_in[l].rearrange("(k p) n -> p k n", p=128)
            for hf in range(2):
                wt, wr = wload(wi[:, :, hf * 1024:(hf + 1) * 1024], lambda a: a[:, 0:2048].rearrange("p (k n) -> p k n", k=2))
                w3 = wt[:, 0:2048].rearrange("p (k n) -> p k n", k=2)
                for m8 in range(8):
                    mc = hf * 8 + m8
                    bank = 2 + (pbank[0] % 4)
                    pbank[0] += 1
                    for kc in range(2):
                        A("pe", (lambda bank, kc, m8, w3: lambda e: e.matmul(PB[bank][:, 0:N], lhsT=w3[:, kc, m8 * 128:(m8 + 1) * 128], rhs=pTb[:, kc, c0:c1],
                                                                             start=(kc == 0), stop=(kc == 1)))(bank, kc, m8, w3),
                          r=wr + ["ptb"], w=pr(bank, 0, N))
                    A("act", (lambda bank, mc: lambda e: e.activation(out=mb[:, mc, c0:c1], in_=PB[bank][:, 0:N], func=AF.Copy))(bank, mc),
                      r=pr(bank, 0, N), w=["mb%d" % mc])
            rms_stats(lambda kc: (mb[:, kc, c0:c1], "mb%d" % kc), c0, c1, 1, 1.0 / D)
            pre_norm(O_PG + l * 32 + 16, c0, c1)

            def evac(mc, ps, psr):
                sg = sgt[mc % 2]
                A("act", lambda e: e.activation(out=sg[:, 0:N], in_=ps, func=AF.Sigmoid), r=psr, w=["p_sg%d" % (mc % 2)])
                A("dve", lambda e: e.scalar_tensor_tensor(out=mb[:, mc, c0:c1], in0=mb[:, mc, c0:c1], scalar=col(O_PG + l * 32 + mc), in1=RS[1][:, 0:N],
                                                          op0=ALU.mult, op1=ALU.mult), r=["mb%d" % mc, "rs1", "spm"], w=["mb%d" % mc])
                A("dve", lambda e: e.tensor_tensor(out=mb[:, mc, c0:c1], in0=mb[:, mc, c0:c1], in1=sg[:, 0:N], op=ALU.mult),
                  r=["mb%d" % mc, "p_sg%d" % (mc % 2)], w=["mb%d" % mc])
                A("dve", lambda e: e.tensor_tensor(out=r[:, mc, c0:c1], in0=r[:, mc, c0:c1], in1=mb[:, mc, c0:c1], op=ALU.add),
                  r=["mb%d" % mc, "r%d" % mc], w=["r%d" % mc])
            proj_fm(pw_g[l], lambda kc: (hn[:, kc, c0:c1], "hn%d" % kc), c0, c1, evac)
        S.barrier()

    RALL = ["r%d" % k for k in range(KC)]
    stages = ["hgrn", "mix0", "ffn0", "ple0", "gmlp", "mix1", "ffn1", "ple1"]
    nst = len(stages) if stop is None else stages.index(stop) + 1

    dump = os.environ.get('MK_DUMP') == '1'
    dbgt = {}
    if dump:
        for st_ in ("mix0", "ffn0", "ple0", "mix1", "ffn1"):
            dbgt[st_] = nc.dram_tensor("dbg_" + st_, [128, 4, 4 * T], F32, kind="ExternalOutput").ap()

    dbgy = nc.dram_tensor("dbg_y", [128, KC, T], F32, kind="ExternalOutput").ap() if os.environ.get('MK_DUMPY') in ('1', '2') else None

    def dump_r(st_, ti):
        if dump and ti > 0:
            A("sp", lambda e: e.dma_start(out=dbgt[st_][:, :, (ti - 1) * T:ti * T], in_=r[:, 0:4, :]), r=RALL[:4], dma="dbg")

    def load_x(tok0):
        A("sp", lambda e: e.dma_start(out=r[:], in_=xT[:, :, tok0:tok0 + T]), w=RALL, dma="xin")

    for ti in range(3 - NPRE, 3):
        load_x(ti * T)
        pre_norm(O_NG + 0, 0, T)
        hgrn(True)
    for ti in range(NEXT):
        tok0 = (3 + ti) * T
        load_x(tok0)
        ra = (0, T)
        rb = (256, T) if ti == 0 else (0, T)
        rc = (384, T) if ti == 0 else (0, T)
        dbg = int(os.environ.get('MK_DBG', '0'))
        if dbg != 1:
            pre_norm(O_NG + 0, *ra)
        if dbg not in (1, 2):
            hgrn(os.environ.get('MK_SO') == '1')
        if nst >= 2:
            proj_fm(hw_out, lambda kc: (yb[:, kc, rb[0]:rb[1]], "yb%d" % kc), rb[0], rb[1], evac_to_mb(*rb))
            post_norm_add(O_NG + 16, *rb)
            dump_r('mix0', ti)
        if nst >= 3:
            pre_norm(O_NG + 32, *rb)
            ffn(0, *rb)
            post_norm_add(O_NG + 48, *rb)
            dump_r('ffn0', ti)
        if nst >= 4:
            ple(0, rb[0], rb[1], ti * T)
            dump_r('ple0', ti)
        if nst >= 5:
            pre_norm(O_NG + 64, *rc)
            gmlp(*rc)
            if os.environ.get('MK_DUMPY') == '2' and ti == 1:
                A("sp", lambda e: e.dma_start(out=dbgy, in_=mb[:]), r=MBALL, dma="dbgy")
            if os.environ.get('MK_DUMPY') == '1' and ti == 1:
                for kc in range(KC):
                    A("dve", (lambda kc: lambda e: e.tensor_copy(out=mb[:, kc, :], in_=yb[:, kc, :]))(kc), r=["yb%d" % kc], w=["mb%d" % kc])
                A("sp", lambda e: e.dma_start(out=dbgy, in_=mb[:]), r=MBALL, dma="dbgy")
        if nst >= 6:
            proj_fm(gw_out, lambda kc: (yb[:, kc, rc[0]:rc[1]], "yb%d" % kc), rc[0], rc[1], evac_to_mb(*rc))
            post_norm_add(O_NG + 80, *rc)
            dump_r('mix1', ti)
        if nst >= 7:
            pre_norm(O_NG + 96, *rc)
            ffn(1, rc[0], rc[1], up_only=(ti == 0))
            if ti == 0:
                hv = HALO[:, 1, :, :].rearrange("p a b -> p (a b)")
                A("dve", lambda e: e.tensor_scalar_mul(out=hv, in0=hv, scalar1=col(O_HM)), r=["halo1", "spm"], w=["halo1"])
            else:
                post_norm_add(O_NG + 112, *rc)
                dump_r('ffn1', ti)
        if nst >= 8 and ti > 0:
            ple(1, rc[0], rc[1], ti * T)
        if ti > 0:
            A("sp", (lambda ti: lambda e: e.dma_start(out=outT[:, :, (ti - 1) * T:ti * T], in_=r[:]))(ti), r=RALL, dma="out")
    print('MK ops:', opcnt[0])
    S.finish("sp")
    S.emit(nc)
    es.close()
    return nc


def _layout(inputs):
    f = lambda a: np.ascontiguousarray(np.asarray(a, dtype=np.float32))
    x = f(inputs["x"])
    p = f(inputs["p"])
    spc = np.zeros((128, NSP), np.float32)
    spc[:, O_NG:O_NG + 128] = f(inputs["norm_g"]).reshape(2, 4, 16, 128).transpose(3, 0, 1, 2).reshape(128, 128)
    spc[:, O_PG:O_PG + 64] = f(inputs["ple_norm_g"]).reshape(2, 2, 16, 128).transpose(3, 0, 1, 2).reshape(128, 64)
    spc[:, O_CW:O_CW + 768] = f(inputs["ffn_conv_w"]).reshape(2, 3, 128, 128).transpose(3, 0, 1, 2).reshape(128, 768)
    spc[:, O_CB:O_CB + 256] = f(inputs["ffn_conv_b"]).reshape(2, 128, 128).transpose(2, 0, 1).reshape(128, 256)
    spc[:, O_LNG:O_LNG + 16] = f(inputs["gmlp_ln_g"]).reshape(16, 128).T
    spc[:, O_LNB:O_LNB + 16] = f(inputs["gmlp_ln_b"]).reshape(16, 128).T
    spc[:, O_GNO] = f(inputs["hgrn_norm_g"]).reshape(128)
    shared = {
        "lbl": f(inputs["hgrn_lb_logits"]),
        "wsT": f(np.asarray(inputs["gmlp_w_s"])[0].transpose(2, 0, 1)),
        "bsd": f(inputs["gmlp_b_s"]).reshape(D),
        "hw_in": f(inputs["hgrn_w_in"])[0], "hw_out": f(inputs["hgrn_w_out"])[0],
        "gw_in": f(inputs["gmlp_w_in"])[0], "gw_out": f(inputs["gmlp_w_out"])[0],
        "w_up": f(inputs["ffn_w_up"]), "w_dn": f(inputs["ffn_w_down"]),
        "pw_in": f(inputs["ple_w_in"]), "pw_g": f(inputs["ple_w_gate"]),
    }
    maps = []
    for c in range(8):
        b, h = c // 2, c % 2
        if h == 1:
            seq = x[b]
            pe = p[:, b, SEQ - EXT:, :]
        else:
            seq = np.concatenate([np.zeros((SEQ // 2, D), np.float32), x[b, :SEQ // 2]], axis=0)
            pe = np.concatenate([np.zeros((2, EXT - SEQ // 2, 256), np.float32), p[:, b, :SEQ // 2, :]], axis=1)
        xTc = np.ascontiguousarray(seq.T.reshape(KC, 128, SEQ).transpose(1, 0, 2))
        pTc = np.ascontiguousarray(pe.transpose(0, 2, 1).reshape(2, 2, 128, EXT).transpose(0, 2, 1, 3))
        s = spc.copy()
        s[:, O_HM] = float(h)
        m = {"xT": xTc, "pT": pTc, "spd": s}
        m.update(shared)
        maps.append(m)
    return maps


_NC_CACHE = {}


def kernel(**inputs):
    stop = os.environ.get("MK_STOP") or None
    if stop not in _NC_CACHE:
        _NC_CACHE[stop] = build(stop)
    nc = _NC_CACHE[stop]
    maps = _layout(inputs)
    res = run_bass_kernel_spmd(nc, maps, core_ids=list(range(8)))
    out = np.empty((4, SEQ, D), np.float32)
    for c in range(8):
        b, h = c // 2, c % 2
        o = np.asarray(res.results[c]["outT"], dtype=np.float32)
        out[b, h * 2048:(h + 1) * 2048, :] = o.transpose(2, 1, 0).reshape(2048, D)
    return out
```

```python
import os
from contextlib import ExitStack
import numpy as np
import concourse.bass as bass
import concourse.mybir as mybir
from concourse.bass_utils import run_bass_kernel_spmd

F32 = mybir.dt.float32
BF16 = mybir.dt.bfloat16
AF = mybir.ActivationFunctionType
ALU = mybir.AluOpType

D = 2048
KC = 16
T = 512
SEQ = 4096
NPRE = int(os.environ.get('MK_NPRE', '3'))
NEXT = int(os.environ.get('MK_NEXT', '5'))
EXT = 5 * T
DFF = 8192
EPS = 1e-6
O_NG = 0
O_PG = 128
O_CW = 192
O_CB = 960
O_LNG = 1216
O_LNB = 1232
O_GNO = 1248
O_HM = 1249
NSP = 1250


class _Op:
    __slots__ = ("fn", "waits", "signal", "dma")

    def __init__(self, fn, dma):
        self.fn = fn
        self.waits = {}
        self.signal = False
        self.dma = dma


class Sched:
    ENGS = ("pe", "act", "dve", "pool", "sp")

    def __init__(self):
        self.ops = {e: [] for e in self.ENGS}
        self.lastw = {}
        self.readers = {}
        self.known = {e: {} for e in self.ENGS}
        self.dma_cnt = {}
        self.floor = {e: set() for e in self.ENGS}

    def add(self, eng, fn, r=(), w=(), dma=None, strict=False):
        op = _Op(fn, dma)
        idx = len(self.ops[eng])
        if dma is not None:
            k = self.dma_cnt.get(dma, 0) + 1
            self.dma_cnt[dma] = k
            token = ("dma", dma, k)
        else:
            token = ("eng", eng, idx)
        deps = set(self.floor[eng])
        self.floor[eng] = set()
        for x in r:
            t = self.lastw.get(x)
            if t is not None:
                deps.add(t)
        for x in w:
            t = self.lastw.get(x)
            if t is not None:
                deps.add(t)
            deps.update(self.readers.get(x, ()))
        kn = self.known[eng]
        for (kind, key, n) in deps:
            if kind == "eng" and key == eng and not strict:
                continue
            if kn.get((kind, key), -1) >= n:
                continue
            kn[(kind, key)] = n
            if op.waits.get((kind, key), -1) < n:
                op.waits[(kind, key)] = n
            if kind == "eng":
                self.ops[key][n].signal = True
        for x in r:
            self.readers.setdefault(x, []).append(token)
        for x in w:
            self.lastw[x] = token
            self.readers[x] = []
        self.ops[eng].append(op)
        return token

    def barrier(self, engs=("pe", "act", "dve")):
        toks = set()
        for e in engs:
            if self.ops[e]:
                toks.add(("eng", e, len(self.ops[e]) - 1))
        for e in engs:
            self.floor[e] |= toks

    def finish(self, eng="sp"):
        op = _Op(None, None)
        for ch, k in self.dma_cnt.items():
            op.waits[("dma", ch)] = k
        self.ops[eng].append(op)

    def emit(self, nc):
        vals = {}
        for e in self.ENGS:
            c = 0
            v = []
            for op in self.ops[e]:
                if op.signal:
                    c += 1
                v.append(c)
            vals[e] = v
        with ExitStack() as es:
            esem = {}
            for e in self.ENGS:
                if any(op.signal for op in self.ops[e]):
                    esem[e] = es.enter_context(nc.semaphore("s_" + e))
            dsem = {ch: es.enter_context(nc.semaphore("d_" + str(ch))) for ch in self.dma_cnt}
            block = es.enter_context(nc.Block())

            def run(e, engine):
                for op in self.ops[e]:
                    for (kind, key), n in op.waits.items():
                        if kind == "eng":
                            engine.wait_ge(esem[key], vals[key][n])
                        else:
                            engine.wait_ge(dsem[key], 16 * n)
                    if op.fn is None:
                        continue
                    ins = op.fn(engine)
                    if op.dma is not None:
                        ins.then_inc(dsem[op.dma], 16)
                    elif op.signal:
                        ins.then_inc(esem[e], 1)

            @block.tensor
            def _(eng):
                run("pe", eng)

            @block.scalar
            def _(eng):
                run("act", eng)

            @block.vector
            def _(eng):
                run("dve", eng)

            @block.gpsimd
            def _(eng):
                run("pool", eng)

            @block.sync
            def _(eng):
                run("sp", eng)


def build(stop=None):
    nc = bass.Bass("TRN2", target_bir_lowering=False)
    S = Sched()
    maxops = int(os.environ.get('MK_MAXOPS', '0'))
    opcnt = [0]

    def A(eng, fn, r=(), w=(), dma=None, strict=False):
        if eng != "sp":
            opcnt[0] += 1
            if maxops and opcnt[0] > maxops:
                return None
        psr_ = [x for x in r if isinstance(x, tuple) and x[0] == "ps"]
        if psr_:
            r = [x for x in r if not (isinstance(x, tuple) and x[0] == "ps")]
            w = list(w) + psr_
        return S.add(eng, fn, r=r, w=w, dma=dma, strict=strict)
    smallset = set(os.environ.get('MK_SMALLW', '').split(','))

    def din(name, shape):
        return nc.dram_tensor(name, shape, F32, kind="ExternalInput").ap()

    xT = din("xT", [128, KC, SEQ])
    pT = din("pT", [2, 128, 2, EXT])
    spd = din("spd", [128, NSP])
    lbl = din("lbl", [3, D])
    wsT = din("wsT", [128, 16, 128])
    bsd = din("bsd", [D])
    hw_in = din("hw_in", [D, 4 * D])
    hw_out = din("hw_out", [1, 1] if "hw_out" in smallset else [D, D])
    gw_in = din("gw_in", [1, 1] if "gw_in" in smallset else [D, 2 * D])
    gw_out = din("gw_out", [1, 1] if "gw_out" in smallset else [D, D])
    w_up = din("w_up", [1, 1] if "w_up" in smallset else [2, D, 2 * DFF])
    w_dn = din("w_dn", [1, 1] if "w_dn" in smallset else [2, DFF, D])
    pw_in = din("pw_in", [1, 1] if "pw_in" in smallset else [2, 256, D])
    pw_g = din("pw_g", [1, 1] if "pw_g" in smallset else [2, D, D])
    outT = nc.dram_tensor("outT", [128, KC, 4 * T], F32, kind="ExternalOutput").ap()

    es = ExitStack()
    uid = [0]

    def un(name):
        uid[0] += 1
        return "%s_%d" % (name, uid[0])

    def sb(name, shape, dt=F32):
        return es.enter_context(nc.sbuf_tensor(name, shape, dt))

    r = sb("r", [128, KC, T])
    hn = sb("hn", [128, KC, T], BF16)
    yb = sb("yb", [128, KC, T], BF16)
    mb = sb("mb", [128, KC, T])
    WS = [sb(f"W{i}", [128, 4096], BF16) for i in range(4)]
    spm = sb("spm", [128, NSP])
    lbc = sb("lbc", [128, D])
    C2 = sb("C2", [128, 16, 128])
    WST = sb("WST", [128, 16, 128], BF16)
    TRI = sb("TRI", [128, 128])
    M2 = sb("M2", [128, 128])
    MNEG = sb("MNEG", [128, 128])
    TRIF = sb("TRIF", [128, 128])
    IND = sb("IND", [128, 32])
    ONES = sb("ONES", [128, 128], BF16)
    IDENT = sb("IDENT", [128, 128], BF16)
    IDF = sb("IDF", [128, 128])
    EPSC = sb("EPSC", [128, 1])
    stf = sb("stf", [128, 16, 128])
    stb = sb("stb", [128, 16, 2, 128], BF16)
    HALO = sb("HALO", [128, 2, 128, 2])
    RS = [sb(f"RS{i}", [128, T]) for i in range(2)]
    SQ = [sb(f"SQ{i}", [128, T], BF16) for i in range(2)]
    pTb = sb("pTb", [128, 2, T], BF16)
    PB = [es.enter_context(nc.psum_tensor(f"PB{i}", [128, 512], F32)) for i in range(7)]
    PH = es.enter_context(nc.psum_tensor("PH", [128, 1024], BF16))

    def pr(bank, c0, c1):
        return [("ps", bank)]

    def prh(par):
        return [("ps", 7)]

    def col(off):
        return spm[:, off:off + 1]

    A("sp", lambda e: e.dma_start(out=spm[:], in_=spd), w=["spm"], dma="c_spm")
    A("pool", lambda e: e.memset(ONES[:], 1.0), w=["ones"])
    A("pool", lambda e: e.memset(EPSC[:], EPS), w=["epsc"])
    A("pool", lambda e: e.memset(IDF[:], 0.0), w=["idf"])
    A("pool", lambda e: e.affine_select(out=IDF[:], in_=IDF[:], pattern=[[-1, 128]], compare_op=ALU.not_equal,
                                        fill=1.0, base=0, channel_multiplier=1), r=["idf"], w=["idf"])
    A("dve", lambda e: e.tensor_copy(out=IDENT[:], in_=IDF[:]), r=["idf"], w=["ident"])
    A("pool", lambda e: e.memset(TRIF[:], 1.0), w=["trif"])
    A("pool", lambda e: e.affine_select(out=TRIF[:], in_=TRIF[:], pattern=[[1, 128]], compare_op=ALU.is_ge,
                                        fill=0.0, base=0, channel_multiplier=-1), r=["trif"], w=["trif"])
    A("pool", lambda e: e.tensor_copy(out=TRI[:], in_=TRIF[:]), r=["trif"], w=["tri"])
    A("pool", lambda e: e.memset(TRI[0:64, 64:128], 0.0), r=["tri"], w=["tri"])
    A("pool", lambda e: e.memset(M2[:], 1.0), w=["m2"])
    A("pool", lambda e: e.affine_select(out=M2[:], in_=M2[:], pattern=[[-1, 128]], compare_op=ALU.is_gt,
                                        fill=0.0, base=0, channel_multiplier=1), r=["m2"], w=["m2"])
    A("pool", lambda e: e.memset(M2[64:128, 0:64], 0.0), r=["m2"], w=["m2"])
    A("dve", lambda e: e.tensor_scalar_mul(out=MNEG[:], in0=TRI[:], scalar1=-1.0), r=["tri"], w=["mneg"])
    A("pool", lambda e: e.memset(IND[:], 0.0), w=["ind"])
    A("pool", lambda e: e.memset(IND[0:64, 0:1], 1.0), r=["ind"], w=["ind"])
    A("pool", lambda e: e.memset(IND[64:128, 1:2], 1.0), r=["ind"], w=["ind"])
    A("pool", lambda e: e.memset(stf[:], 0.0), w=["stf%d" % h for h in range(16)])
    A("pool", lambda e: e.memset(stb[:], 0.0), w=["stb%d_%d" % (h, c) for h in range(16) for c in range(2)])
    A("pool", lambda e: e.memset(HALO[:], 0.0), w=["halo0", "halo1"])

    mbv = mb[:].rearrange("p a b -> p (a b)")
    MBALL = ["mb%d" % k for k in range(16)]
    for i in range(3):
        A("sp", (lambda i: lambda e: e.dma_start(out=mbv[:, i * D:(i + 1) * D], in_=lbl[i].partition_broadcast(128)))(i),
          w=MBALL[4 * i:4 * i + 4], dma="c_lb%d" % i)
        A("act", (lambda i: lambda e: e.activation(out=mbv[:, i * D:(i + 1) * D], in_=mbv[:, i * D:(i + 1) * D], func=AF.Exp))(i),
          r=MBALL[4 * i:4 * i + 4], w=MBALL[4 * i:4 * i + 4])
    A("dve", lambda e: e.tensor_tensor(out=mbv[:, D:2 * D], in0=mbv[:, D:2 * D], in1=mbv[:, 2 * D:3 * D], op=ALU.add),
      r=MBALL[4:12], w=MBALL[4:8])
    A("dve", lambda e: e.tensor_tensor(out=mbv[:, D:2 * D], in0=mbv[:, D:2 * D], in1=mbv[:, 0:D], op=ALU.add),
      r=MBALL[0:8], w=MBALL[4:8])
    A("dve", lambda e: e.reciprocal(out=mbv[:, D:2 * D], in_=mbv[:, D:2 * D]), r=MBALL[4:8], w=MBALL[4:8])
    A("dve", lambda e: e.tensor_tensor(out=lbc[:], in0=mbv[:, 0:D], in1=mbv[:, D:2 * D], op=ALU.mult),
      r=MBALL[0:8], w=["lbc"])
    wsf = mbv[:, 3 * D:4 * D].rearrange("p (g t) -> p g t", g=16)
    A("sp", lambda e: e.dma_start(out=wsf, in_=wsT), w=MBALL[12:16], dma="c_ws")
    A("sp", lambda e: e.dma_start(out=C2[:].rearrange("p g t -> p (g t)"), in_=bsd.partition_broadcast(128)),
      w=["c2"], dma="c_bs")
    for g in range(16):
        A("dve", (lambda g: lambda e: e.tensor_tensor(out=WST[:, g, :], in0=wsf[:, g, :], in1=TRIF[:], op=ALU.mult))(g),
          r=MBALL[12:16] + ["trif"], w=["wst"])
    for g in range(16):
        bank = 1 + (g % 2)
        A("pe", (lambda g, bank: lambda e: e.matmul(PB[bank][:, 0:128], lhsT=ONES[:], rhs=WST[:, g, :], start=True, stop=True))(g, bank),
          r=["ones", "wst"], w=pr(bank, 0, 128))
        A("dve", (lambda g, bank: lambda e: e.scalar_tensor_tensor(out=C2[:, g, :], in0=PB[bank][:, 0:128], scalar=col(O_LNB + g),
                                                                   in1=C2[:, g, :], op0=ALU.mult, op1=ALU.add))(g, bank),
          r=pr(bank, 0, 128) + ["spm", "c2"], w=["c2"])
    S.barrier()

    wctr = [0]

    def wload(src_ap, view):
        i = wctr[0] % 4
        wctr[0] += 1
        dst = view(WS[i][:])
        names = ["W%da" % i, "W%db" % i]
        if len(src_ap.shape) == 4:
            for sg_ in range(src_ap.shape[2]):
                A("pool", (lambda d_, s_: lambda e: e.dma_start(out=d_, in_=s_))(dst[:, :, sg_, :], src_ap[:, :, sg_, :]),
                  w=[names[sg_]], dma="w%d" % i)
        else:
            A("pool", lambda e: e.dma_start(out=dst, in_=src_ap), w=names, dma="w%d" % i)
        return WS[i], names

    def v16(ap):
        return ap.rearrange("p (k n) -> p k n", k=16)

    def rms_stats(src, c0, c1, ri, scale):
        N = c1 - c0
        for kc in range(KC):
            ap, rn = src(kc)
            sq = SQ[kc % 2]
            A("act", (lambda ap, sq: lambda e: e.activation(out=sq[:, 0:N], in_=ap, func=AF.Square))(ap, sq),
              r=[rn], w=["sq%d" % (kc % 2)])
            A("pe", (lambda sq, kc: lambda e: e.matmul(PB[0][:, 0:N], lhsT=ONES[:], rhs=sq[:, 0:N], start=(kc == 0), stop=(kc == KC - 1)))(sq, kc),
              r=["sq%d" % (kc % 2), "ones"], w=pr(0, 0, N))
        A("act", lambda e: e.activation(out=RS[ri][:, 0:N], in_=PB[0][:, 0:N], func=AF.Ln, scale=scale, bias=EPSC[:]),
          r=pr(0, 0, N) + ["epsc"], w=["rs%d" % ri])
        A("act", lambda e: e.activation(out=RS[ri][:, 0:N], in_=RS[ri][:, 0:N], func=AF.Exp, scale=-0.5),
          r=["rs%d" % ri], w=["rs%d" % ri])

    def pre_norm(goff, c0, c1):
        N = c1 - c0
        rms_stats(lambda kc: (r[:, kc, c0:c1], "r%d" % kc), c0, c1, 0, 1.0 / D)
        for kc in range(KC):
            A("dve", (lambda kc: lambda e: e.scalar_tensor_tensor(out=hn[:, kc, c0:c1], in0=r[:, kc, c0:c1], scalar=col(goff + kc),
                                                                  in1=RS[0][:, 0:N], op0=ALU.mult, op1=ALU.mult))(kc),
              r=["r%d" % kc, "rs0", "spm"], w=["hn%d" % kc])

    def post_norm_add(goff, c0, c1):
        N = c1 - c0
        rms_stats(lambda kc: (mb[:, kc, c0:c1], "mb%d" % kc), c0, c1, 1, 1.0 / D)
        for kc in range(KC):
            A("dve", (lambda kc: lambda e: e.scalar_tensor_tensor(out=mb[:, kc, c0:c1], in0=mb[:, kc, c0:c1], scalar=col(goff + kc),
                                                                  in1=RS[1][:, 0:N], op0=ALU.mult, op1=ALU.mult))(kc),
              r=["mb%d" % kc, "rs1", "spm"], w=["mb%d" % kc])
            A("dve", (lambda kc: lambda e: e.tensor_tensor(out=r[:, kc, c0:c1], in0=r[:, kc, c0:c1], in1=mb[:, kc, c0:c1], op=ALU.add))(kc),
              r=["mb%d" % kc, "r%d" % kc], w=["r%d" % kc])

    pbank = [0]

    def proj_fm(wsrc, src, c0, c1, evac):
        N = c1 - c0
        for q in range(8):
            wt, wr = wload(wsrc.rearrange("(k p) n -> p k n", p=128)[:, :, q * 256:(q + 1) * 256], v16)
            w3 = v16(wt[:])
            for m2 in range(2):
                mc = q * 2 + m2
                bank = 2 + (pbank[0] % 4)
                pbank[0] += 1
                for kc in range(KC):
                    ap, rn = src(kc)
                    A("pe", (lambda bank, w3, kc, m2, ap: lambda e: e.matmul(PB[bank][:, 0:N], lhsT=w3[:, kc, m2 * 128:(m2 + 1) * 128], rhs=ap,
                                                                             start=(kc == 0), stop=(kc == KC - 1)))(bank, w3, kc, m2, ap),
                      r=wr + [rn], w=pr(bank, 0, N))
                evac(mc, PB[bank][:, 0:N], pr(bank, 0, N))

    def evac_to_mb(c0, c1):
        def f(mc, ps, psr):
            A("act", lambda e: e.activation(out=mb[:, mc, c0:c1], in_=ps, func=AF.Copy), r=psr, w=["mb%d" % mc])
        return f

    def hgrn(state_only):
        with ExitStack() as hs:
            def tb(name, shape, dt=F32):
                return hs.enter_context(nc.sbuf_tensor(un(name), shape, dt))
            GS = tb("h_gs", [128, T])
            tf = {}
            for par in range(2):
                for nm in ("sg", "ns", "lg", "eb", "enb", "ed", "y1", "ln"):
                    tf[nm, par] = tb(f"h_{nm}{par}", [128, 128])
                for nm in ("q", "k", "kd", "v0", "v1", "qT", "kT", "sc", "sq"):
                    tf[nm, par] = tb(f"h_{nm}{par}", [128, 128], BF16)
                tf["ebl", par] = tb(f"h_ebl{par}", [128, 2])
            win = hw_in.rearrange("(k p) (s n) -> p k s n", p=128, s=4)
            def head(h):
                hc = slice(h * 128, (h + 1) * 128)
                wB3 = wBr = None
                if state_only:
                    wA, wAr = wload(win[:, :, 1:3, hc], lambda a: a.rearrange("p (k s n) -> p k s n", k=16, s=2))
                    wA3 = v16(wA[:])
                else:
                    wA, wAr = wload(win[:, :, 0:2, hc], lambda a: a.rearrange("p (k s n) -> p k s n", k=16, s=2))
                    wB, wBr = wload(win[:, :, 2:4, hc], lambda a: a.rearrange("p (k s n) -> p k s n", k=16, s=2))
                    wA3 = v16(wA[:])
                    wB3 = v16(wB[:])
                    for kc in range(KC):
                        A("pe", (lambda kc, wB3: lambda e: e.matmul(PB[4][:, :], lhsT=wB3[:, kc, 128:256], rhs=hn[:, kc, :],
                                                                    start=(kc == 0), stop=(kc == KC - 1)))(kc, wB3),
                          r=wBr + ["hn%d" % kc], w=pr(4, 0, 512))
                    A("act", lambda e: e.activation(out=GS[:], in_=PB[4][:, :], func=AF.Silu), r=pr(4, 0, 512), w=["h_gs"])
                hblock(h, 0, hc, wA3, wAr, wB3, wBr, "proj")
                for blk in range(4):
                    if blk + 1 < 4:
                        hblock(h, blk + 1, hc, wA3, wAr, wB3, wBr, "proj")
                    hblock(h, blk, hc, wA3, wAr, wB3, wBr, "rest")

            def hblock(h, blk, hc, wA3, wAr, wB3, wBr, phase):
                if True:
                    par = blk % 2
                    bc = slice(blk * 128, (blk + 1) * 128)
                    PJ = PB[2 + par]
                    t = lambda nm: tf[nm, par]
                    R = lambda nm: "h_%s%d" % (nm, par)
                    if phase == "rest":
                        pass
                    elif state_only:
                        for kc in range(KC):
                            A("pe", (lambda kc, PJ, wA3, bc: lambda e: e.matmul(PJ[:, 128:384], lhsT=hn[:, kc, bc], rhs=wA3[:, kc, :],
                                                                                start=(kc == 0), stop=(kc == KC - 1)))(kc, PJ, wA3, bc),
                              r=wAr + ["hn%d" % kc], w=pr(2 + par, 128, 384))
                    else:
                        for kc in range(KC):
                            A("pe", (lambda kc, PJ, wA3, bc: lambda e: e.matmul(PJ[:, 0:256], lhsT=hn[:, kc, bc], rhs=wA3[:, kc, :],
                                                                                start=(kc == 0), stop=(kc == KC - 1)))(kc, PJ, wA3, bc),
                              r=wAr + ["hn%d" % kc], w=pr(2 + par, 0, 256))
                        for kc in range(KC):
                            A("pe", (lambda kc, PJ, wB3, bc: lambda e: e.matmul(PJ[:, 256:384], lhsT=hn[:, kc, bc], rhs=wB3[:, kc, 0:128],
                                                                                start=(kc == 0), stop=(kc == KC - 1)))(kc, PJ, wB3, bc),
                              r=wBr + ["hn%d" % kc], w=pr(2 + par, 256, 384))
                    if phase == "proj":
                        return
                    sg, ns, lg = t("sg"), t("ns"), t("lg")
                    A("act", (lambda sg, PJ: lambda e: e.activation(out=sg[:], in_=PJ[:, 128:256], func=AF.Sigmoid))(sg, PJ),
                      r=pr(2 + par, 128, 256), w=[R("sg")])
                    A("act", (lambda ns, PJ: lambda e: e.activation(out=ns[:], in_=PJ[:, 128:256], func=AF.Sigmoid, scale=-1.0))(ns, PJ),
                      r=pr(2 + par, 128, 256), w=[R("ns")])
                    A("dve", (lambda ns, hc: lambda e: e.tensor_tensor(out=ns[:], in0=ns[:], in1=lbc[:, hc], op=ALU.mult))(ns, hc),
                      r=[R("ns"), "lbc"], w=[R("ns")])
                    A("dve", (lambda sg, ns: lambda e: e.tensor_tensor(out=sg[:], in0=sg[:], in1=ns[:], op=ALU.add))(sg, ns),
                      r=[R("ns"), R("sg")], w=[R("sg")])
                    A("act", (lambda lg, sg: lambda e: e.activation(out=lg[:], in_=sg[:], func=AF.Ln))(lg, sg),
                      r=[R("sg")], w=[R("lg")])
                    BDc = par * 256
                    if not state_only:
                        A("pe", (lambda lg, BDc: lambda e: e.matmul(PB[5][:, BDc:BDc + 128], lhsT=TRI[:], rhs=lg[:], start=True, stop=True))(lg, BDc),
                          r=[R("lg"), "tri"], w=pr(5, BDc, BDc + 128))
                    A("pe", (lambda lg, BDc: lambda e: e.matmul(PB[5][:, BDc + 128:BDc + 256], lhsT=M2[:], rhs=lg[:], start=True, stop=True))(lg, BDc),
                      r=[R("lg"), "m2"], w=pr(5, BDc + 128, BDc + 256))
                    EBc = 256 + par * 128
                    A("pe", (lambda lg, EBc: lambda e: e.matmul(PB[0][:, EBc:EBc + 32], lhsT=lg[:], rhs=IND[:], start=True, stop=True))(lg, EBc),
                      r=[R("lg"), "ind"], w=pr(0, EBc, EBc + 32))
                    ed, ebl, kd = t("ed"), t("ebl"), t("kd")
                    vm = [t("v0"), t("v1")]
                    A("act", (lambda ed, BDc: lambda e: e.activation(out=ed[:], in_=PB[5][:, BDc + 128:BDc + 256], func=AF.Exp))(ed, BDc),
                      r=pr(5, BDc + 128, BDc + 256), w=[R("ed")])
                    A("act", (lambda ebl, EBc: lambda e: e.activation(out=ebl[:], in_=PB[0][:, EBc:EBc + 2], func=AF.Exp))(ebl, EBc),
                      r=pr(0, EBc, EBc + 2), w=[R("ebl")])
                    A("dve", (lambda kd, sg, ed: lambda e: e.scalar_tensor_tensor(out=kd[:], in0=sg[:], scalar=-1.0, in1=ed[:],
                                                                                  op0=ALU.add, op1=ALU.mult))(kd, sg, ed),
                      r=[R("sg"), R("ed")], w=[R("kd")])
                    for c in range(2):
                        A("act", (lambda vv, PJ, c: lambda e: e.activation(out=vv[:], in_=PJ[:, 256:384], func=AF.Copy, scale=IND[:, c:c + 1]))(vm[c], PJ, c),
                          r=pr(2 + par, 256, 384) + ["ind"], w=[R("v%d" % c)])
                    if not state_only:
                        eb, enb, q, k, qT, kT, sc, sq, y1, ln = (t(n) for n in ("eb", "enb", "q", "k", "qT", "kT", "sc", "sq", "y1", "ln"))
                        A("act", (lambda eb, BDc: lambda e: e.activation(out=eb[:], in_=PB[5][:, BDc:BDc + 128], func=AF.Exp))(eb, BDc),
                          r=pr(5, BDc, BDc + 128), w=[R("eb")])
                        A("act", (lambda enb, BDc: lambda e: e.activation(out=enb[:], in_=PB[5][:, BDc:BDc + 128], func=AF.Exp, scale=-1.0))(enb, BDc),
                          r=pr(5, BDc, BDc + 128), w=[R("enb")])
                        A("dve", (lambda q, PJ, eb: lambda e: e.tensor_tensor(out=q[:], in0=PJ[:, 0:128], in1=eb[:], op=ALU.mult))(q, PJ, eb),
                          r=pr(2 + par, 0, 128) + [R("eb")], w=[R("q")])
                        A("dve", (lambda k, sg, enb: lambda e: e.scalar_tensor_tensor(out=k[:], in0=sg[:], scalar=-1.0, in1=enb[:],
                                                                                      op0=ALU.add, op1=ALU.mult))(k, sg, enb),
                          r=[R("sg"), R("enb")], w=[R("k")])
                        A("pe", (lambda q: lambda e: e.transpose(PH[:, par * 256:par * 256 + 128], q[:], IDENT[:]))(q),
                          r=[R("q"), "ident"], w=prh(par))
                        A("pe", (lambda k: lambda e: e.transpose(PH[:, par * 256 + 128:par * 256 + 256], k[:], IDENT[:]))(k),
                          r=[R("k"), "ident"], w=prh(par))
                        A("dve", (lambda qT: lambda e: e.tensor_copy(out=qT[:], in_=PH[:, par * 256:par * 256 + 128]))(qT),
                          r=prh(par), w=[R("qT")])
                        A("act", (lambda kT: lambda e: e.activation(out=kT[:], in_=PH[:, par * 256 + 128:par * 256 + 256], func=AF.Copy))(kT),
                          r=prh(par), w=[R("kT")])
                        SCc = par * 256
                        A("pe", (lambda kT, qT, SCc: lambda e: e.matmul(PB[6][:, SCc:SCc + 128], lhsT=kT[:], rhs=qT[:], start=True, stop=True))(kT, qT, SCc),
                          r=[R("kT"), R("qT")], w=pr(6, SCc, SCc + 128))
                        A("dve", (lambda sc, SCc: lambda e: e.tensor_tensor(out=sc[:], in0=PB[6][:, SCc:SCc + 128], in1=MNEG[:], op=ALU.mult))(sc, SCc),
                          r=pr(6, SCc, SCc + 128) + ["mneg"], w=[R("sc")])
                    OTc = par * 256 + 128
                    for c in range(2):
                        cs = slice(c * 64, (c + 1) * 64)
                        KVc = c * 128
                        if not state_only:
                            A("pe", (lambda v, sc, cs, OTc, c: lambda e: e.matmul(PB[6][:, OTc + c * 64:OTc + c * 64 + 64], lhsT=v[:], rhs=sc[:, cs],
                                                                                  start=True, stop=False))(vm[c], sc, cs, OTc, c),
                              r=[R("v%d" % c), R("sc")], w=pr(6, OTc, OTc + 128))
                            A("pe", (lambda qT, cs, OTc, c: lambda e: e.matmul(PB[6][:, OTc + c * 64:OTc + c * 64 + 64], lhsT=stb[:, h, c, :], rhs=qT[:, cs],
                                                                               start=False, stop=True))(qT, cs, OTc, c),
                              r=[R("qT"), "stb%d_%d" % (h, c)], w=pr(6, OTc, OTc + 128))
                        A("pe", (lambda kd, v, cs, KVc: lambda e: e.matmul(PB[1][:, KVc:KVc + 128], lhsT=kd[:], rhs=v[:], start=True, stop=True))(kd, vm[c], cs, KVc),
                          r=[R("kd"), R("v%d" % c)], w=pr(1, KVc, KVc + 128))
                        A("dve", (lambda ebl, c, KVc: lambda e: e.scalar_tensor_tensor(out=stf[:, h, :], in0=stf[:, h, :], scalar=ebl[:, c:c + 1],
                                                                                       in1=PB[1][:, KVc:KVc + 128], op0=ALU.mult, op1=ALU.subtract))(ebl, c, KVc),
                          r=pr(1, KVc, KVc + 128) + [R("ebl"), "stf%d" % h], w=["stf%d" % h])
                        nxt = (c + 1) % 2
                        A("act", (lambda nxt: lambda e: e.activation(out=stb[:, h, nxt, :], in_=stf[:, h, :], func=AF.Copy))(nxt),
                          r=["stf%d" % h], w=["stb%d_%d" % (h, nxt)])
                    if not state_only:
                        A("act", (lambda sq, OTc: lambda e: e.activation(out=sq[:], in_=PB[6][:, OTc:OTc + 128], func=AF.Square))(sq, OTc),
                          r=pr(6, OTc, OTc + 128), w=[R("sq")])
                        SSc = par * 128
                        A("pe", (lambda sq, SSc: lambda e: e.matmul(PB[0][:, SSc:SSc + 128], lhsT=ONES[:], rhs=sq[:], start=True, stop=True))(sq, SSc),
                          r=[R("sq"), "ones"], w=pr(0, SSc, SSc + 128))
                        A("act", (lambda ln, SSc: lambda e: e.activation(out=ln[:], in_=PB[0][:, SSc:SSc + 128], func=AF.Ln, scale=1.0 / 128, bias=EPSC[:]))(ln, SSc),
                          r=pr(0, SSc, SSc + 128) + ["epsc"], w=[R("ln")])
                        A("act", (lambda ln: lambda e: e.activation(out=ln[:], in_=ln[:], func=AF.Exp, scale=-0.5))(ln),
                          r=[R("ln")], w=[R("ln")])
                        A("dve", (lambda y1, ln, OTc: lambda e: e.scalar_tensor_tensor(out=y1[:], in0=PB[6][:, OTc:OTc + 128], scalar=col(O_GNO), in1=ln[:],
                                                                                       op0=ALU.mult, op1=ALU.mult))(y1, ln, OTc),
                          r=pr(6, OTc, OTc + 128) + [R("ln"), "spm"], w=[R("y1")])
                        A("dve", (lambda y1, bc: lambda e: e.tensor_tensor(out=yb[:, h, bc], in0=y1[:], in1=GS[:, bc], op=ALU.mult))(y1, bc),
                          r=[R("y1"), "h_gs"], w=["yb%d" % h])

            for h in range(int(os.environ.get('MK_HEADS', '16'))):
                head(h)
        S.barrier()

    def gmlp(c0, c1):
        N = c1 - c0
        b0, b1 = c0 // 128, c1 // 128
        with ExitStack() as hs:
            def tb(name, shape, dt=F32):
                return hs.enter_context(nc.sbuf_tensor(un(name), shape, dt))
            vln = tb("g_vln", [128, 4, D], BF16)
            tu = [tb(f"g_u{i}", [128, T]) for i in range(2)]
            tm = [tb(f"g_m{i}", [128, T]) for i in range(2)]
            st = tb("g_st", [128, 4, 4])
            mv = tb("g_mv", [128, 4, 2])
            rsd = tb("g_rsd", [128, 4])
            win = gw_in.rearrange("(k p) n -> p k n", p=128)
            vf = mb[:].rearrange("p a b -> p (a b)")
            for q8 in range(8):
                wt, wr = wload(win[:, :, D + q8 * 256:D + (q8 + 1) * 256], v16)
                w3 = v16(wt[:])
                for blk in range(b0, b1):
                    bank = 2 + (pbank[0] % 4)
                    pbank[0] += 1
                    bc = slice(blk * 128, (blk + 1) * 128)
                    for kc in range(KC):
                        A("pe", (lambda bank, kc, bc, w3: lambda e: e.matmul(PB[bank][:, 0:256], lhsT=hn[:, kc, bc], rhs=w3[:, kc, :],
                                                                             start=(kc == 0), stop=(kc == KC - 1)))(bank, kc, bc, w3),
                          r=wr + ["hn%d" % kc], w=pr(bank, 0, 256))
                    off = blk * D + q8 * 256
                    A("act", (lambda bank, off: lambda e: e.activation(out=vf[:, off:off + 256], in_=PB[bank][:, 0:256], func=AF.Gelu))(bank, off),
                      r=pr(bank, 0, 256), w=["mb%d" % (off // 512)])
            for blk in range(b0, b1):
                mbr = ["mb%d" % (blk * 4 + i) for i in range(4)]
                A("dve", (lambda blk: lambda e: e.memset(st[:, blk, :], 0.0))(blk), w=["g_st%d" % blk])
                A("act", (lambda blk: lambda e: e.activation(out=vln[:, blk, :], in_=vf[:, blk * D:(blk + 1) * D], func=AF.Copy,
                                                             accum_out=st[:, blk, 0:1]))(blk),
                  r=mbr + ["g_st%d" % blk], w=["g_vln%d" % blk, "g_st%d" % blk])
                A("act", (lambda blk: lambda e: e.activation(out=vln[:, blk, :], in_=vf[:, blk * D:(blk + 1) * D], func=AF.Square,
                                                             accum_out=st[:, blk, 1:2]))(blk),
                  r=mbr + ["g_st%d" % blk], w=["g_vln%d" % blk, "g_st%d" % blk])
                A("dve", (lambda blk: lambda e: e.tensor_scalar_mul(out=mv[:, blk, 0:1], in0=st[:, blk, 0:1], scalar1=1.0 / D))(blk),
                  r=["g_st%d" % blk], w=["g_mv%d" % blk])
                A("dve", (lambda blk: lambda e: e.tensor_tensor(out=st[:, blk, 2:3], in0=mv[:, blk, 0:1], in1=mv[:, blk, 0:1], op=ALU.mult))(blk),
                  r=["g_mv%d" % blk], w=["g_st%d" % blk], strict=True)
                A("dve", (lambda blk: lambda e: e.scalar_tensor_tensor(out=mv[:, blk, 1:2], in0=st[:, blk, 1:2], scalar=1.0 / D, in1=st[:, blk, 2:3],
                                                                       op0=ALU.mult, op1=ALU.subtract))(blk),
                  r=["g_st%d" % blk], w=["g_mv%d" % blk], strict=True)
                A("act", (lambda blk: lambda e: e.activation(out=rsd[:, blk:blk + 1], in_=mv[:, blk, 1:2], func=AF.Ln, bias=EPSC[:]))(blk),
                  r=["g_mv%d" % blk, "epsc"], w=["g_rsd%d" % blk])
                A("act", (lambda blk: lambda e: e.activation(out=rsd[:, blk:blk + 1], in_=rsd[:, blk:blk + 1], func=AF.Exp, scale=-0.5))(blk),
                  r=["g_rsd%d" % blk], w=["g_rsd%d" % blk], strict=True)
                A("dve", (lambda blk: lambda e: e.tensor_scalar(out=vln[:, blk, :], in0=vf[:, blk * D:(blk + 1) * D], scalar1=mv[:, blk, 0:1],
                                                                scalar2=rsd[:, blk:blk + 1], op0=ALU.subtract, op1=ALU.mult))(blk),
                  r=mbr + ["g_mv%d" % blk, "g_rsd%d" % blk], w=["g_vln%d" % blk], strict=True)
                if os.environ.get('MK_DUMPY') == '2' and blk == 0:
                    A("dve", (lambda blk: lambda e: e.tensor_copy(out=vf[:, blk * D:(blk + 1) * D], in_=vln[:, blk, :]))(blk),
                      r=["g_vln%d" % blk], w=mbr)
                    A("dve", (lambda blk: lambda e: e.tensor_copy(out=vf[:, (blk + 1) * D:(blk + 1) * D + 2], in_=mv[:, blk, :]))(blk),
                      r=["g_mv%d" % blk], w=["mb%d" % ((blk + 1) * 4)])
                    A("dve", (lambda blk: lambda e: e.tensor_copy(out=vf[:, (blk + 1) * D + 2:(blk + 1) * D + 3], in_=rsd[:, blk:blk + 1]))(blk),
                      r=["g_rsd%d" % blk], w=["mb%d" % ((blk + 1) * 4)])
            for g in range(16):
                if g % 2 == 0:
                    wt, wr = wload(win[:, :, (g // 2) * 256:(g // 2 + 1) * 256], v16)
                    w3 = v16(wt[:])
                ub = 4 + (g % 2)
                for kc in range(KC):
                    A("pe", (lambda ub, kc, w3, g: lambda e: e.matmul(PB[ub][:, 0:N], lhsT=w3[:, kc, (g % 2) * 128:(g % 2 + 1) * 128], rhs=hn[:, kc, c0:c1],
                                                                      start=(kc == 0), stop=(kc == KC - 1)))(ub, kc, w3, g),
                      r=wr + ["hn%d" % kc], w=pr(ub, 0, N))
                u = tu[g % 2]
                A("act", (lambda u, ub: lambda e: e.activation(out=u[:, 0:N], in_=PB[ub][:, 0:N], func=AF.Gelu))(u, ub),
                  r=pr(ub, 0, N), w=["g_u%d" % (g % 2)])
                sbk = 6 if g % 2 == 0 else 1
                m = tm[g % 2]
                for blk in range(b0, b1):
                    o = (blk - b0) * 128
                    A("pe", (lambda sbk, o, blk, g: lambda e: e.matmul(PB[sbk][:, o:o + 128], lhsT=vln[:, blk, g * 128:(g + 1) * 128], rhs=WST[:, g, :],
                                                                       start=True, stop=True))(sbk, o, blk, g),
                      r=["g_vln%d" % blk, "wst"], w=pr(sbk, o, o + 128))
                    A("dve", (lambda sbk, o, m, g: lambda e: e.scalar_tensor_tensor(out=m[:, o:o + 128], in0=PB[sbk][:, o:o + 128], scalar=col(O_LNG + g),
                                                                                    in1=C2[:, g, :], op0=ALU.mult, op1=ALU.add))(sbk, o, m, g),
                      r=pr(sbk, o, o + 128) + ["c2", "spm"], w=["g_m%d" % (g % 2)])
                A("dve", (lambda m, u, g: lambda e: e.tensor_tensor(out=yb[:, g, c0:c1], in0=m[:, 0:N], in1=u[:, 0:N], op=ALU.mult))(m, u, g),
                  r=["g_m%d" % (g % 2), "g_u%d" % (g % 2)], w=["yb%d" % g])
        S.barrier()

    def ffn(l, c0, c1, up_only=False):
        N = c1 - c0
        with ExitStack() as hs:
            def tb(name, shape, dt=F32):
                return hs.enter_context(nc.sbuf_tensor(un(name), shape, dt))
            HS = {(z, p): tb(f"f_hs{z}{p}", [128, 2 + T]) for z in range(2) for p in range(2)}
            TA = {(z, p): tb(f"f_ta{z}{p}", [128, T]) for z in range(2) for p in range(2)}
            ACTB = [tb(f"f_act{p}", [128, 2, T], BF16) for p in range(2)]
            wu = w_up[l].rearrange("(k p) (s n) -> p k s n", p=128, s=2)
            wd = w_dn[l].rearrange("(k p) n -> p k n", p=128)
            for ci in range(64):
                par = ci % 2
                rd = ci // 2
                wt, wr = wload(wu[:, :, :, ci * 128:(ci + 1) * 128], lambda a: a.rearrange("p (k s n) -> p k s n", k=16, s=2))
                w3 = v16(wt[:])
                if ci % 2 == 0 and not up_only:
                    wdt, wdr = wload(wd[:, rd * 2:rd * 2 + 2, :], lambda a: a.rearrange("p (k n) -> p k n", k=2))
                    wd3 = wdt[:].rearrange("p (k n) -> p k n", k=2)
                for z in range(2):
                    hci = z * 64 + ci
                    bank = 2 + z * 2 + par
                    for kc in range(KC):
                        A("pe", (lambda bank, kc, w3, z: lambda e: e.matmul(PB[bank][:, 0:N], lhsT=w3[:, kc, z * 128:(z + 1) * 128], rhs=hn[:, kc, c0:c1],
                                                                            start=(kc == 0), stop=(kc == KC - 1)))(bank, kc, w3, z),
                          r=wr + ["hn%d" % kc], w=pr(bank, 0, N))
                    hsb = HS[z, par]
                    ta = TA[z, par]
                    hr = "f_hs%d%d" % (z, par)
                    tr = "f_ta%d%d" % (z, par)
                    A("act", (lambda hsb, hci: lambda e: e.activation(out=hsb[:, 0:2], in_=HALO[:, l, hci, :], func=AF.Copy))(hsb, hci),
                      r=["halo%d" % l], w=[hr])
                    A("act", (lambda hsb, bank: lambda e: e.activation(out=hsb[:, 2:2 + N], in_=PB[bank][:, 0:N], func=AF.Copy))(hsb, bank),
                      r=pr(bank, 0, N), w=[hr])
                    A("act", (lambda hsb, hci: lambda e: e.activation(out=HALO[:, l, hci, :], in_=hsb[:, N:N + 2], func=AF.Copy))(hsb, hci),
                      r=[hr], w=["halo%d" % l])
                    if up_only:
                        continue
                    cw = O_CW + l * 384 + hci
                    A("act", (lambda ta, bank, cw, hci: lambda e: e.activation(out=ta[:, 0:N], in_=PB[bank][:, 0:N], func=AF.Identity,
                                                                               scale=col(cw + 256), bias=col(O_CB + l * 128 + hci)))(ta, bank, cw, hci),
                      r=pr(bank, 0, N) + ["spm"], w=[tr])
                    A("dve", (lambda ta, hsb, cw: lambda e: e.scalar_tensor_tensor(out=ta[:, 0:N], in0=hsb[:, 1:1 + N], scalar=col(cw + 128), in1=ta[:, 0:N],
                                                                                   op0=ALU.mult, op1=ALU.add))(ta, hsb, cw),
                      r=[hr, tr, "spm"], w=[tr])
                    A("dve", (lambda ta, hsb, cw: lambda e: e.scalar_tensor_tensor(out=ta[:, 0:N], in0=hsb[:, 0:N], scalar=col(cw), in1=ta[:, 0:N],
                                                                                   op0=ALU.mult, op1=ALU.add))(ta, hsb, cw),
                      r=[hr, tr, "spm"], w=[tr])
                if up_only:
                    continue
                tg, tv = TA[0, par], TA[1, par]
                A("act", (lambda tg: lambda e: e.activation(out=tg[:, 0:N], in_=tg[:, 0:N], func=AF.Gelu_apprx_tanh))(tg),
                  r=["f_ta0%d" % par], w=["f_ta0%d" % par])
                ab = ACTB[rd % 2]
                A("dve", (lambda ab, tg, tv, par: lambda e: e.tensor_tensor(out=ab[:, par, 0:N], in0=tg[:, 0:N], in1=tv[:, 0:N], op=ALU.mult))(ab, tg, tv, par),
                  r=["f_ta0%d" % par, "f_ta1%d" % par], w=["f_act%d" % (rd % 2)])
                if ci % 2 == 1:
                    for mc in range(KC):
                        bank = 6 if mc % 2 == 0 else 1
                        for a in range(2):
                            A("pe", (lambda bank, a, mc, wd3, ab: lambda e: e.matmul(PB[bank][:, 0:N], lhsT=wd3[:, a, mc * 128:(mc + 1) * 128], rhs=ab[:, a, 0:N],
                                                                                     start=(a == 0), stop=(a == 1)))(bank, a, mc, wd3, ab),
                              r=wdr + ["f_act%d" % (rd % 2)], w=pr(bank, 0, N))
                        if rd == 0:
                            A("act", (lambda bank, mc: lambda e: e.activation(out=mb[:, mc, c0:c1], in_=PB[bank][:, 0:N], func=AF.Copy))(bank, mc),
                              r=pr(bank, 0, N), w=["mb%d" % mc])
                        else:
                            A("dve", (lambda bank, mc: lambda e: e.tensor_tensor(out=mb[:, mc, c0:c1], in0=mb[:, mc, c0:c1], in1=PB[bank][:, 0:N], op=ALU.add))(bank, mc),
                              r=pr(bank, 0, N) + ["mb%d" % mc], w=["mb%d" % mc])
        S.barrier()

    def ple(l, c0, c1, tok0):
        N = c1 - c0
        with ExitStack() as hs:
            sgt = [hs.enter_context(nc.sbuf_tensor(un(f"p_sg{i}"), [128, T], F32)) for i in range(2)]
            A("pool", lambda e: e.dma_start(out=pTb[:, :, c0:c1], in_=pT[l, :, :, tok0 + c0:tok0 + c1]), w=["ptb"], dma="ptb")
            wi = pw_in[l].rearrange("(k p) n -> p k n", p=128)
            for hf in range(2):
                wt, wr = wload(wi[:, :, hf * 1024:(hf + 1) * 1024], lambda a: a[:, 0:2048].rearrange("p (k n) -> p k n", k=2))
                w3 = wt[:, 0:2048].rearrange("p (k n) -> p k n", k=2)
                for m8 in range(8):
                    mc = hf * 8 + m8
                    bank = 2 + (pbank[0] % 4)
                    pbank[0] += 1
                    for kc in range(2):
                        A("pe", (lambda bank, kc, m8, w3: lambda e: e.matmul(PB[bank][:, 0:N], lhsT=w3[:, kc, m8 * 128:(m8 + 1) * 128], rhs=pTb[:, kc, c0:c1],
                                                                             start=(kc == 0), stop=(kc == 1)))(bank, kc, m8, w3),
                          r=wr + ["ptb"], w=pr(bank, 0, N))
                    A("act", (lambda bank, mc: lambda e: e.activation(out=mb[:, mc, c0:c1], in_=PB[bank][:, 0:N], func=AF.Copy))(bank, mc),
                      r=pr(bank, 0, N), w=["mb%d" % mc])
            rms_stats(lambda kc: (mb[:, kc, c0:c1], "mb%d" % kc), c0, c1, 1, 1.0 / D)
            pre_norm(O_PG + l * 32 + 16, c0, c1)

            def evac(mc, ps, psr):
                sg = sgt[mc % 2]
                A("act", lambda e: e.activation(out=sg[:, 0:N], in_=ps, func=AF.Sigmoid), r=psr, w=["p_sg%d" % (mc % 2)])
                A("dve", lambda e: e.scalar_tensor_tensor(out=mb[:, mc, c0:c1], in0=mb[:, mc, c0:c1], scalar=col(O_PG + l * 32 + mc), in1=RS[1][:, 0:N],
                                                          op0=ALU.mult, op1=ALU.mult), r=["mb%d" % mc, "rs1", "spm"], w=["mb%d" % mc])
                A("dve", lambda e: e.tensor_tensor(out=mb[:, mc, c0:c1], in0=mb[:, mc, c0:c1], in1=sg[:, 0:N], op=ALU.mult),
                  r=["mb%d" % mc, "p_sg%d" % (mc % 2)], w=["mb%d" % mc])
                A("dve", lambda e: e.tensor_tensor(out=r[:, mc, c0:c1], in0=r[:, mc, c0:c1], in1=mb[:, mc, c0:c1], op=ALU.add),
                  r=["mb%d" % mc, "r%d" % mc], w=["r%d" % mc])
            proj_fm(pw_g[l], lambda kc: (hn[:, kc, c0:c1], "hn%d" % kc), c0, c1, evac)
        S.barrier()

    RALL = ["r%d" % k for k in range(KC)]
    stages = ["hgrn", "mix0", "ffn0", "ple0", "gmlp", "mix1", "ffn1", "ple1"]
    nst = len(stages) if stop is None else stages.index(stop) + 1

    dump = os.environ.get('MK_DUMP') == '1'
    dbgt = {}
    if dump:
        for st_ in ("mix0", "ffn0", "ple0", "mix1", "ffn1"):
            dbgt[st_] = nc.dram_tensor("dbg_" + st_, [128, 4, 4 * T], F32, kind="ExternalOutput").ap()

    dbgy = nc.dram_tensor("dbg_y", [128, KC, T], F32, kind="ExternalOutput").ap() if os.environ.get('MK_DUMPY') in ('1', '2') else None

    def dump_r(st_, ti):
        if dump and ti > 0:
            A("sp", lambda e: e.dma_start(out=dbgt[st_][:, :, (ti - 1) * T:ti * T], in_=r[:, 0:4, :]), r=RALL[:4], dma="dbg")

    def load_x(tok0):
        A("sp", lambda e: e.dma_start(out=r[:], in_=xT[:, :, tok0:tok0 + T]), w=RALL, dma="xin")

    for ti in range(3 - NPRE, 3):
        load_x(ti * T)
        pre_norm(O_NG + 0, 0, T)
        hgrn(True)
    for ti in range(NEXT):
        tok0 = (3 + ti) * T
        load_x(tok0)
        ra = (0, T)
        rb = (256, T) if ti == 0 else (0, T)
        rc = (384, T) if ti == 0 else (0, T)
        dbg = int(os.environ.get('MK_DBG', '0'))
        if dbg != 1:
            pre_norm(O_NG + 0, *ra)
        if dbg not in (1, 2):
            hgrn(os.environ.get('MK_SO') == '1')
        if nst >= 2:
            proj_fm(hw_out, lambda kc: (yb[:, kc, rb[0]:rb[1]], "yb%d" % kc), rb[0], rb[1], evac_to_mb(*rb))
            post_norm_add(O_NG + 16, *rb)
            dump_r('mix0', ti)
        if nst >= 3:
            pre_norm(O_NG + 32, *rb)
            ffn(0, *rb)
            post_norm_add(O_NG + 48, *rb)
            dump_r('ffn0', ti)
        if nst >= 4:
            ple(0, rb[0], rb[1], ti * T)
            dump_r('ple0', ti)
        if nst >= 5:
            pre_norm(O_NG + 64, *rc)
            gmlp(*rc)
            if os.environ.get('MK_DUMPY') == '2' and ti == 1:
                A("sp", lambda e: e.dma_start(out=dbgy, in_=mb[:]), r=MBALL, dma="dbgy")
            if os.environ.get('MK_DUMPY') == '1' and ti == 1:
                for kc in range(KC):
                    A("dve", (lambda kc: lambda e: e.tensor_copy(out=mb[:, kc, :], in_=yb[:, kc, :]))(kc), r=["yb%d" % kc], w=["mb%d" % kc])
                A("sp", lambda e: e.dma_start(out=dbgy, in_=mb[:]), r=MBALL, dma="dbgy")
        if nst >= 6:
            proj_fm(gw_out, lambda kc: (yb[:, kc, rc[0]:rc[1]], "yb%d" % kc), rc[0], rc[1], evac_to_mb(*rc))
            post_norm_add(O_NG + 80, *rc)
            dump_r('mix1', ti)
        if nst >= 7:
            pre_norm(O_NG + 96, *rc)
            ffn(1, rc[0], rc[1], up_only=(ti == 0))
            if ti == 0:
                hv = HALO[:, 1, :, :].rearrange("p a b -> p (a b)")
                A("dve", lambda e: e.tensor_scalar_mul(out=hv, in0=hv, scalar1=col(O_HM)), r=["halo1", "spm"], w=["halo1"])
            else:
                post_norm_add(O_NG + 112, *rc)
                dump_r('ffn1', ti)
        if nst >= 8 and ti > 0:
            ple(1, rc[0], rc[1], ti * T)
        if ti > 0:
            A("sp", (lambda ti: lambda e: e.dma_start(out=outT[:, :, (ti - 1) * T:ti * T], in_=r[:]))(ti), r=RALL, dma="out")
    print('MK ops:', opcnt[0])
    S.finish("sp")
    S.emit(nc)
    es.close()
    return nc


def _layout(inputs):
    f = lambda a: np.ascontiguousarray(np.asarray(a, dtype=np.float32))
    x = f(inputs["x"])
    p = f(inputs["p"])
    spc = np.zeros((128, NSP), np.float32)
    spc[:, O_NG:O_NG + 128] = f(inputs["norm_g"]).reshape(2, 4, 16, 128).transpose(3, 0, 1, 2).reshape(128, 128)
    spc[:, O_PG:O_PG + 64] = f(inputs["ple_norm_g"]).reshape(2, 2, 16, 128).transpose(3, 0, 1, 2).reshape(128, 64)
    spc[:, O_CW:O_CW + 768] = f(inputs["ffn_conv_w"]).reshape(2, 3, 128, 128).transpose(3, 0, 1, 2).reshape(128, 768)
    spc[:, O_CB:O_CB + 256] = f(inputs["ffn_conv_b"]).reshape(2, 128, 128).transpose(2, 0, 1).reshape(128, 256)
    spc[:, O_LNG:O_LNG + 16] = f(inputs["gmlp_ln_g"]).reshape(16, 128).T
    spc[:, O_LNB:O_LNB + 16] = f(inputs["gmlp_ln_b"]).reshape(16, 128).T
    spc[:, O_GNO] = f(inputs["hgrn_norm_g"]).reshape(128)
    shared = {
        "lbl": f(inputs["hgrn_lb_logits"]),
        "wsT": f(np.asarray(inputs["gmlp_w_s"])[0].transpose(2, 0, 1)),
        "bsd": f(inputs["gmlp_b_s"]).reshape(D),
        "hw_in": f(inputs["hgrn_w_in"])[0], "hw_out": f(inputs["hgrn_w_out"])[0],
        "gw_in": f(inputs["gmlp_w_in"])[0], "gw_out": f(inputs["gmlp_w_out"])[0],
        "w_up": f(inputs["ffn_w_up"]), "w_dn": f(inputs["ffn_w_down"]),
        "pw_in": f(inputs["ple_w_in"]), "pw_g": f(inputs["ple_w_gate"]),
    }
    maps = []
    for c in range(8):
        b, h = c // 2, c % 2
        if h == 1:
            seq = x[b]
            pe = p[:, b, SEQ - EXT:, :]
        else:
            seq = np.concatenate([np.zeros((SEQ // 2, D), np.float32), x[b, :SEQ // 2]], axis=0)
            pe = np.concatenate([np.zeros((2, EXT - SEQ // 2, 256), np.float32), p[:, b, :SEQ // 2, :]], axis=1)
        xTc = np.ascontiguousarray(seq.T.reshape(KC, 128, SEQ).transpose(1, 0, 2))
        pTc = np.ascontiguousarray(pe.transpose(0, 2, 1).reshape(2, 2, 128, EXT).transpose(0, 2, 1, 3))
        s = spc.copy()
        s[:, O_HM] = float(h)
        m = {"xT": xTc, "pT": pTc, "spd": s}
        m.update(shared)
        maps.append(m)
    return maps


_NC_CACHE = {}


def kernel(**inputs):
    stop = os.environ.get("MK_STOP") or None
    if stop not in _NC_CACHE:
        _NC_CACHE[stop] = build(stop)
    nc = _NC_CACHE[stop]
    maps = _layout(inputs)
    res = run_bass_kernel_spmd(nc, maps, core_ids=list(range(8)))
    out = np.empty((4, SEQ, D), np.float32)
    for c in range(8):
        b, h = c // 2, c % 2
        o = np.asarray(res.results[c]["outT"], dtype=np.float32)
        out[b, h * 2048:(h + 1) * 2048, :] = o.transpose(2, 1, 0).reshape(2048, D)
    return out
```
